# Optimizing a Trainium2 kernel written in Bass

```python
import jax, jax.numpy as jnp
from jax import lax
import numpy as np

D_MODEL = 1024
BATCH = 32
SEQ = 2048
DEPTH = 4
DEC_BATCH = 4
DEC_SEQ = 4096
PAST_LEN = 128

N_HEADS = 8
QK_NOPE = 128
QK_ROPE = 64
V_HEAD = 128
Q_LORA = 512
KV_LORA = 256
ATTN_WIDTH = N_HEADS * V_HEAD
QK_HEAD = QK_NOPE + QK_ROPE
ROPE_BASE = 10000.0
Q_BLOCK = 128
N_FOURIER_GROUPS = 4
FOURIER_GROUP = 128
FOURIER_WIDTH = N_FOURIER_GROUPS * FOURIER_GROUP
D_FF = 2816
CONV_WIDTH = 3
EPS = 1e-6
N_ADA = 6
OFF_Q = 0
OFF_KV = OFF_Q + Q_LORA
OFF_KR = OFF_KV + KV_LORA
OFF_F = OFF_KR + QK_ROPE
OFF_G = OFF_F + FOURIER_WIDTH
IN_WIDTH = OFF_G + 2 * D_MODEL

kernel_name = 'hybrid_mla_fnet_convffn_encoder'


def rms_norm(x, g):
    xf = x.astype(jnp.float32)
    y = xf * lax.rsqrt(jnp.mean(xf * xf, axis=-1, keepdims=True) + EPS)
    return (y * g.astype(jnp.float32)).astype(x.dtype)


def rope_tables(seq_len):
    half = QK_ROPE // 2
    inv = 1.0 / (ROPE_BASE ** (jnp.arange(half, dtype=jnp.float32) / half))
    ang = jnp.arange(seq_len, dtype=jnp.float32)[:, None] * inv[None, :]
    return jnp.cos(ang), jnp.sin(ang)


def apply_rope(x, cos, sin):
    x1, x2 = jnp.split(x.astype(jnp.float32), 2, axis=-1)
    c = cos[None, :, None, :]
    s = sin[None, :, None, :]
    return jnp.concatenate([x1 * c - x2 * s, x2 * c + x1 * s], axis=-1).astype(x.dtype)


def bidir_attention(q, k, v):
    B, S, H, Dq = q.shape
    nb = S // Q_BLOCK
    scale = Dq ** -0.5
    qb = q.reshape(B, nb, Q_BLOCK, H, Dq).transpose(1, 0, 2, 3, 4)

    def block(qi):
        s = jnp.einsum('bqhd,bkhd->bhqk', qi, k).astype(jnp.float32) * scale
        p = jax.nn.softmax(s, axis=-1).astype(v.dtype)
        return jnp.einsum('bhqk,bkhd->bqhd', p, v)

    o = lax.map(block, qb)
    return o.transpose(1, 0, 2, 3, 4).reshape(B, S, H * v.shape[-1])


def fourier_mix(u):
    B, S, _ = u.shape
    ug = u.reshape(B, S, N_FOURIER_GROUPS, FOURIER_GROUP).astype(jnp.float32)
    f = jnp.fft.fft2(ug, axes=(1, 3), norm='ortho').real
    return f.reshape(B, S, FOURIER_WIDTH).astype(u.dtype)


def centred_dwconv(h, w, b):
    S = h.shape[1]
    hp = jnp.pad(h, ((0, 0), (1, 1), (0, 0)))
    return hp[:, :S] * w[0] + hp[:, 1:S + 1] * w[1] + hp[:, 2:] * w[2] + b


def mixer_block(h, cos, sin, w_in, g_q, w_q_b, g_kv, w_kv_b, w_attn_o, w_four, w_out):
    B, S, _ = h.shape
    z = h @ w_in
    c_q = rms_norm(z[..., OFF_Q:OFF_KV], g_q)
    c_kv = rms_norm(z[..., OFF_KV:OFF_KR], g_kv)
    k_rope = z[..., OFF_KR:OFF_F].reshape(B, S, 1, QK_ROPE)
    u_f = z[..., OFF_F:OFF_G]
    gate_a, gate_b = jnp.split(z[..., OFF_G:], 2, axis=-1)

    q = (c_q @ w_q_b).reshape(B, S, N_HEADS, QK_HEAD)
    q = jnp.concatenate([q[..., :QK_NOPE], apply_rope(q[..., QK_NOPE:], cos, sin)], axis=-1)
    kv = (c_kv @ w_kv_b).reshape(B, S, N_HEADS, QK_NOPE + V_HEAD)
    k_nope, v = kv[..., :QK_NOPE], kv[..., QK_NOPE:]
    k_rope = jnp.broadcast_to(apply_rope(k_rope, cos, sin), (B, S, N_HEADS, QK_ROPE))
    k = jnp.concatenate([k_nope, k_rope], axis=-1)

    branch_a = bidir_attention(q, k, v) @ w_attn_o
    branch_b = fourier_mix(u_f) @ w_four
    merged = jax.nn.sigmoid(gate_a) * branch_a + jax.nn.sigmoid(gate_b) * branch_b
    return merged @ w_out


def conv_ffn(h, w_up, w_conv, b_conv, w_down):
    up = centred_dwconv(h @ w_up, w_conv, b_conv)
    a, b = jnp.split(up, 2, axis=-1)
    return (jax.nn.gelu(a, approximate=True) * b) @ w_down


def trunk(x, c, w_ada, b_ada, g_mix_pre, g_mix_post, w_in, g_q, w_q_b, g_kv, w_kv_b,
          w_attn_o, w_four, w_out, g_ffn_pre, g_ffn_post, w_up, w_conv, b_conv, w_down):
    cos, sin = rope_tables(x.shape[1])
    for l in range(DEPTH):
        mod = jax.nn.silu(c) @ w_ada[l] + b_ada[l]
        sh1, sc1, gt1, sh2, sc2, gt2 = [m[:, None, :] for m in jnp.split(mod, N_ADA, axis=-1)]
        h = rms_norm(x, g_mix_pre[l]) * (1 + sc1) + sh1
        y = mixer_block(h, cos, sin, w_in[l], g_q[l], w_q_b[l], g_kv[l], w_kv_b[l],
                        w_attn_o[l], w_four[l], w_out[l])
        x = x + gt1 * rms_norm(y, g_mix_post[l])
        h = rms_norm(x, g_ffn_pre[l]) * (1 + sc2) + sh2
        y = conv_ffn(h, w_up[l], w_conv[l], b_conv[l], w_down[l])
        x = x + gt2 * rms_norm(y, g_ffn_post[l])
    return x


def setup_inputs(seed: int = 0) -> dict:
    key = jax.random.key(seed)
    ks = jax.random.split(key, 24)
    f32 = jnp.float32

    def dense(k, fan_in, fan_out, scale=1.0):
        return jax.random.normal(k, (DEPTH, fan_in, fan_out), f32) * (scale * fan_in ** -0.5)

    def gain(k, dim):
        return 1.0 + 0.02 * jax.random.normal(k, (DEPTH, dim), f32)

    return {
        'x_prompt': jax.random.normal(ks[0], (BATCH, SEQ, D_MODEL), f32),
        'x_sample': jax.random.normal(ks[1], (DEC_BATCH, DEC_SEQ, D_MODEL), f32),
        'c_prompt': jax.random.normal(ks[2], (BATCH, D_MODEL), f32),
        'c_sample': jax.random.normal(ks[3], (DEC_BATCH, D_MODEL), f32),
        'w_ada': dense(ks[4], D_MODEL, N_ADA * D_MODEL, 0.5),
        'b_ada': 0.02 * jax.random.normal(ks[5], (DEPTH, N_ADA * D_MODEL), f32),
        'g_mix_pre': gain(ks[6], D_MODEL),
        'g_mix_post': gain(ks[7], D_MODEL),
        'w_in': dense(ks[8], D_MODEL, IN_WIDTH),
        'g_q': gain(ks[9], Q_LORA),
        'w_q_b': dense(ks[10], Q_LORA, N_HEADS * QK_HEAD),
        'g_kv': gain(ks[11], KV_LORA),
        'w_kv_b': dense(ks[12], KV_LORA, N_HEADS * (QK_NOPE + V_HEAD)),
        'w_attn_o': dense(ks[13], ATTN_WIDTH, D_MODEL),
        'w_four': dense(ks[14], FOURIER_WIDTH, D_MODEL),
        'w_out': dense(ks[15], D_MODEL, D_MODEL),
        'g_ffn_pre': gain(ks[16], D_MODEL),
        'g_ffn_post': gain(ks[17], D_MODEL),
        'w_up': dense(ks[18], D_MODEL, 2 * D_FF),
        'w_conv': jax.random.normal(ks[19], (DEPTH, CONV_WIDTH, 2 * D_FF), f32) * (CONV_WIDTH ** -0.5),
        'b_conv': 0.02 * jax.random.normal(ks[20], (DEPTH, 2 * D_FF), f32),
        'w_down': dense(ks[21], D_FF, D_MODEL),
    }


def reference(x_prompt, x_sample, c_prompt, c_sample, w_ada, b_ada, g_mix_pre, g_mix_post,
              w_in, g_q, w_q_b, g_kv, w_kv_b, w_attn_o, w_four, w_out, g_ffn_pre, g_ffn_post,
              w_up, w_conv, b_conv, w_down):
    y_prompt = trunk(x_prompt, c_prompt, w_ada, b_ada, g_mix_pre, g_mix_post, w_in, g_q, w_q_b,
                     g_kv, w_kv_b, w_attn_o, w_four, w_out, g_ffn_pre, g_ffn_post,
                     w_up, w_conv, b_conv, w_down)
    y_sample = trunk(x_sample, c_sample, w_ada, b_ada, g_mix_pre, g_mix_post, w_in, g_q, w_q_b,
                     g_kv, w_kv_b, w_attn_o, w_four, w_out, g_ffn_pre, g_ffn_post,
                     w_up, w_conv, b_conv, w_down)
    return (y_prompt, y_sample)
```

```python
import contextlib
import numpy as np
import ml_dtypes
import concourse.bass as bass
import concourse.mybir as mybir
from concourse.bass_utils import run_bass_kernel_spmd

F32 = mybir.dt.float32
BF16 = mybir.dt.bfloat16
AF = mybir.ActivationFunctionType
ALU = mybir.AluOpType

D = 1024
NH = 8
DFF = 2816
NCH = 44
WIN_W = 3456
EPS = 1e-6
N_CORES = 8


class TL:
    __slots__ = ("sem", "count", "name")

    def __init__(self, sem, name):
        self.sem = sem
        self.count = 0
        self.name = name


class Buf:
    __slots__ = ("w", "r", "name", "x")

    def __init__(self, name="", x=False):
        self.w = None
        self.r = {}
        self.name = name
        self.x = x


class MB:
    def __init__(self):
        self.d = {}

    def w(self, en):
        if en not in self.d:
            self.d[en] = Buf(en)
        return [self.d[en]]

    def all(self):
        return list(self.d.values())


class Eng:
    def __init__(self, name, h, sem):
        self.name = name
        self.h = h
        self.tl = TL(sem, name)
        self.seen = {}


class KB:
    def __init__(self, nc):
        self.nc = nc
        self.E = {}
        for name, h in [("pe", nc.tensor), ("act", nc.scalar), ("dve", nc.vector),
                        ("pool", nc.gpsimd), ("sp", nc.sync)]:
            self.E[name] = Eng(name, h, nc.alloc_semaphore("s_" + name))
        self.dsems = []
        self.nins = 0

    def new_dsem(self, name):
        tl = TL(self.nc.alloc_semaphore(name), name)
        self.dsems.append(tl)
        return tl

    @staticmethod
    def _deps(reads, writes, own=None):
        deps = {}
        for b in reads:
            if b.w is not None:
                tl, v = b.w
                if deps.get(tl, 0) < v:
                    deps[tl] = v
            if b.x:
                for tl, v in b.r.items():
                    if tl is not own and deps.get(tl, 0) < v:
                        deps[tl] = v
        for b in writes:
            if b.w is not None:
                tl, v = b.w
                if tl is not own and deps.get(tl, 0) < v:
                    deps[tl] = v
            for tl, v in b.r.items():
                if deps.get(tl, 0) < v:
                    deps[tl] = v
        return deps

    def _wait(self, e, deps):
        for tl, v in deps.items():
            if tl is e.tl and e.name == "pe":
                continue
            if e.seen.get(tl, 0) < v:
                e.h.wait_ge(tl.sem, v)
                e.seen[tl] = v

    @staticmethod
    def _exp(lst, en):
        out = []
        for b in lst:
            if isinstance(b, MB):
                out.extend(b.all() if en is None else b.w(en))
            else:
                out.append(b)
        return out

    def op(self, en, fn, reads=(), writes=()):
        reads = self._exp(reads, None)
        writes = self._exp(writes, en)
        e = self.E[en]
        self._wait(e, self._deps(reads, writes, e.tl))
        ins = fn(e.h)
        e.tl.count += 1
        ins.then_inc(e.tl.sem, 1)
        v = e.tl.count
        for b in reads:
            b.r[e.tl] = v
        for b in writes:
            b.w = (e.tl, v)
            b.r = {}
        self.nins += 1
        return ins

    def dma(self, q, out, in_, dsem, reads=(), writes=(), **kw):
        reads = self._exp(reads, None)
        writes = self._exp(writes, "dma")
        e = self.E[q]
        self._wait(e, self._deps(reads, writes, dsem))
        ins = e.h.dma_start(out=out, in_=in_, **kw)
        ins.then_inc(dsem.sem, 16)
        dsem.count += 16
        v = dsem.count
        for b in reads:
            b.r[dsem] = v
        for b in writes:
            b.w = (dsem, v)
            b.r = {}
        self.nins += 1
        return ins

    def barrier(self):
        tls = [e.tl for e in self.E.values()] + self.dsems
        for e in self.E.values():
            for tl in tls:
                if tl is e.tl and e.name in ("pe", "sp"):
                    continue
                if tl.count > e.seen.get(tl, 0):
                    e.h.wait_ge(tl.sem, tl.count)
                    e.seen[tl] = tl.count


class Pool:
    def __init__(self, K, st, name, n, shape, dtype, dma=True):
        self.t = [st.enter_context(K.nc.sbuf_tensor(K.nm(f"{name}{i}"), shape, dtype)) for i in range(n)]
        self.b = [Buf(f"{name}{i}") for i in range(n)]
        self.d = [K.dsem(f"{name}{i}") for i in range(n)] if dma else None
        self.i = 0
        self.n = n

    def next(self):
        i = self.i
        self.i = (i + 1) % self.n
        return self.t[i], self.b[i], (self.d[i] if self.d else None)


class Prog:
    def __init__(self, cfg):
        self.cfg = cfg
        self.L = cfg["depth"]
        self.slots = cfg["slots"]
        self.SP = cfg["seg"]
        self.SMAX = max(self.slots)
        self.NTOK = sum(self.slots)
        self.NSEG = self.NTOK // self.SP
        self.dbg = cfg.get("dbg")
        self.nc = bass.Bass("TRN2", target_bir_lowering=False)
        self.K = KB(self.nc)
        self._dsem_cache = {}
        self.K.dsem = self.dsem
        self._uid = 0
        self.SQ = cfg.get("store_q", "pool")
        self.K.nm = self.nm
        self.build()

    def dsem(self, name):
        if name not in self._dsem_cache:
            self._dsem_cache[name] = self.K.new_dsem("d_" + name)
        return self._dsem_cache[name]

    def nm(self, n):
        self._uid += 1
        return f"{n}_{self._uid}"

    def din(self, name, shape, dt=F32):
        return self.nc.dram_tensor(name, list(shape), dt, kind="ExternalInput").ap()

    def dscr(self, name, shape, dt=BF16):
        kind = "ExternalOutput" if (self.dbg and name in self.dbg) else "Internal"
        return self.nc.dram_tensor(name, list(shape), dt, kind=kind).ap()

    def build(self):
        nc, K, L = self.nc, self.K, self.L
        NTOK, NSEG, SMAX = self.NTOK, self.NSEG, self.SMAX
        self.x_all = self.din("x_all", [NTOK, D])
        self.c_all = self.din("c_all", [NSEG, D])
        self.rope_t = self.din("rope_t", [128, NTOK])
        self.ident_d = self.din("ident", [128, 128])
        self.sel_d = self.din("sel", [128, 128])
        self.amask_d = self.din("amask", [128, SMAX // 128, SMAX // 512])
        self.bfac_d = self.din("bfac", [128, 1])
        self.csc_d = {}
        self.dft_c = {}
        self.dft_s = {}
        for S in sorted(set(self.slots)):
            self.csc_d[S] = self.din(f"csc{S}", [128, 256], BF16)
            self.dft_c[S] = self.din(f"dftc{S}", [S, S], BF16)
            self.dft_s[S] = self.din(f"dfts{S}", [S, S], BF16)
        w = {}
        w["w_ada"] = self.din("w_ada", [L, D, 6 * D])
        w["b_ada"] = self.din("b_ada", [L, 6 * D])
        w["g_mix_pre"] = self.din("g_mix_pre", [L, D])
        w["g_mix_post"] = self.din("g_mix_post", [L, D])
        w["w_in"] = self.din("w_in", [L, D, 3392])
        w["g_q"] = self.din("g_q", [L, 512])
        w["w_q_b"] = self.din("w_q_b", [L, 512, 1536])
        w["g_kv"] = self.din("g_kv", [L, 256])
        w["w_kv_b"] = self.din("w_kv_b", [L, 256, 2048])
        w["w_attn_o"] = self.din("w_attn_o", [L, D, D])
        w["w_four"] = self.din("w_four", [L, 512, D])
        w["w_out"] = self.din("w_out", [L, D, D])
        w["g_ffn_pre"] = self.din("g_ffn_pre", [L, D])
        w["g_ffn_post"] = self.din("g_ffn_post", [L, D])
        w["w_up"] = self.din("w_up", [L, D, 2 * DFF])
        w["w_conv"] = self.din("w_conv", [L, 3, 2 * DFF])
        w["b_conv"] = self.din("b_conv", [L, 2 * DFF])
        w["w_down"] = self.din("w_down", [L, DFF, D])
        self.w = w
        self.y_all = self.nc.dram_tensor("y_all", [NTOK, D], F32, kind="ExternalOutput").ap()
        self.xb = self.dscr("xb", [NTOK, D], F32)
        self.xa = self.dscr("xa", [NTOK, D], F32)
        self.WADA = self.dscr("WADA", [L, D, 6 * D])
        self.WIN = self.dscr("WIN", [L, D, WIN_W])
        self.WQ = self.dscr("WQ", [L, 512, 2048])
        self.WK = self.dscr("WK", [L, 256, 1024])
        self.WV = self.dscr("WV", [L, 256, 1024])
        self.WAO = self.dscr("WAO", [L, D, D])
        self.WF = self.dscr("WF", [L, 512, D])
        self.WO = self.dscr("WO", [L, D, D])
        self.WUP = self.dscr("WUP", [L, D, 2 * DFF])
        self.WDN = self.dscr("WDN", [L, DFF, D])
        self.MODS = self.dscr("MODS", [L, NSEG, 6 * D], F32)
        self.QS = self.dscr("QS", [NH, 256, SMAX])
        self.KS = self.dscr("KS", [NH, 128, SMAX])
        self.KKS = self.dscr("KKS", [128, SMAX])
        self.VS = self.dscr("VS", [SMAX, D])
        self.GS = self.dscr("GS", [16, 128, SMAX])
        self.YZS = self.dscr("YZS", [SMAX, D])
        self.ATS = self.dscr("ATS", [NH, 128, SMAX])
        self.H2S = self.dscr("H2S", [8, 128, SMAX + 2])

        with contextlib.ExitStack() as st:
            self.perm(st)
            self.prepass()
            K.barrier()
            self.prologue()
            K.barrier()
            off = 0
            for si, S in enumerate(self.slots):
                for l in range(L):
                    self.slot_layer(si, S, off, l)
                off += S
            K.barrier()

    def perm(self, st):
        nc, L, NSEG = self.nc, self.L, self.NSEG
        A = lambda n, s, d=F32: st.enter_context(nc.sbuf_tensor(self.nm(n), s, d))
        self.ident = A("ident_s", [128, 128])
        self.sel = A("sel_s", [128, 128])
        self.ones = A("ones_s", [128, 128], BF16)
        self.amask = A("amask_s", [128, self.SMAX // 128, self.SMAX // 512])
        self.bfac = A("bfac_s", [128, 1])
        self.b_const = Buf("const")
        self.gpre = A("gpre", [128, L, 8])
        self.gfpre = A("gfpre", [128, L, 8])
        self.gq = A("gq", [128, L, 4])
        self.gkv = A("gkv", [128, L, 2])
        self.wconv = A("wconv", [128, L, 3 * NCH])
        self.bconv = A("bconv", [128, L, NCH])
        self.modF = A("modF", [128, L, 48, NSEG])
        self.A1 = A("A1", [128, L, NSEG, 8])
        self.A2 = A("A2", [128, L, NSEG, 8])
        self.b_vec = Buf("vec")
        self.ps = st.enter_context(nc.psum_tensor("ps", [128, 8, 512], F32))
        self.pb = [Buf(f"ps{i}", x=True) for i in range(8)]
        self.NW = 4

    def prepass(self):
        K, w, L = self.K, self.w, self.L
        ds = self.dsem("pre")
        b = Buf("prew")
        self.b_wscr = b

        ncp = [0]

        def cp(dst, src):
            K.dma("pool", dst, src, ds, writes=[b])
            ncp[0] += 1
            if ncp[0] % 2 == 0:
                e = K.E["pool"]
                e.h.wait_ge(ds.sem, ds.count)
                e.seen[ds] = ds.count

        for l in range(L):
            for r in range(0, D, 256):
                cp(self.WADA[l, r:r + 256, :], w["w_ada"][l, r:r + 256, :])
                cp(self.WUP[l, r:r + 256, :], w["w_up"][l, r:r + 256, :])
                cp(self.WIN[l, r:r + 256, 0:832], w["w_in"][l, r:r + 256, 0:832])
                cp(self.WIN[l, r:r + 256, 832:864], w["w_in"][l, r:r + 256, 800:832])
                cp(self.WIN[l, r:r + 256, 864:896], w["w_in"][l, r:r + 256, 768:800])
                cp(self.WIN[l, r:r + 256, 896:WIN_W], w["w_in"][l, r:r + 256, 832:3392])
                cp(self.WAO[l, r:r + 256, :], w["w_attn_o"][l, r:r + 256, :])
                cp(self.WO[l, r:r + 256, :], w["w_out"][l, r:r + 256, :])
            for r in range(0, DFF, 256):
                cp(self.WDN[l, r:r + 256, :], w["w_down"][l, r:r + 256, :])
            cp(self.WF[l], w["w_four"][l])
            wq_d = self.WQ[l].rearrange("k (h c) -> k h c", h=NH)
            wq_s = w["w_q_b"][l].rearrange("k (h c) -> k h c", h=NH)
            wkv = w["w_kv_b"][l].rearrange("k (h c) -> k h c", h=NH)
            wk_d = self.WK[l].rearrange("k (h c) -> k h c", h=NH)
            wv_d = self.WV[l].rearrange("k (h c) -> k h c", h=NH)
            for h in range(NH):
                cp(wq_d[:, h, 0:192], wq_s[:, h, 0:192])
                cp(wq_d[:, h, 192:224], wq_s[:, h, 160:192])
                cp(wq_d[:, h, 224:256], wq_s[:, h, 128:160])
                cp(wk_d[:, h, :], wkv[:, h, 0:128])
                cp(wv_d[:, h, :], wkv[:, h, 128:256])

    def wload(self, src_ap, view):
        t, b, d = self.wpool.next()
        n = 1
        for s in view:
            n *= s
        dst = t[:, 0:n]
        if len(view) == 2:
            dst = dst.rearrange("p (a b) -> p a b", a=view[0])
        if len(view) == 2 and view[0] * 128 > 512:
            hk = view[0] // 2
            self.K.dma("sp", dst[:, 0:hk, :], src_ap[:, 0:hk, :], d, writes=[b])
            self.K.dma("sp", dst[:, hk:, :], src_ap[:, hk:, :], d, writes=[b])
        else:
            self.K.dma("sp", dst, src_ap, d, writes=[b])
        return dst, b

    def fm_load(self, dst, src2d, n, stg, stg_b, ds):
        K = self.K
        K.dma("sp", stg[0:n, :], src2d, ds, writes=[stg_b])
        K.op("pe", lambda e: e.transpose(out=self.ps[:, 7, 0:n], in_=stg[0:n, :], identity=self.ident[0:n, 0:n]),
             reads=[stg_b, self.b_const], writes=[self.pb[7]])
        K.op("dve", lambda e: e.tensor_copy(out=dst, in_=self.ps[:, 7, 0:n]), reads=[self.pb[7]], writes=[self.b_vec])

    def prologue(self):
        nc, K, L, NSEG, w = self.nc, self.K, self.L, self.NSEG, self.w
        ds = self.dsem("misc")
        with contextlib.ExitStack() as st:
            A = lambda n, s, d=F32: st.enter_context(nc.sbuf_tensor(self.nm(n), s, d))
            self.wpool = Pool(K, st, "wst", self.NW, [128, 4096], BF16)
            dcn = self.dsem("c_id")
            K.dma("sp", self.ident[:], self.ident_d[:, :], dcn, writes=[self.b_const])
            K.dma("sp", self.sel[:], self.sel_d[:, :], dcn, writes=[self.b_const])
            K.dma("sp", self.amask[:], self.amask_d[:, :, :], dcn, writes=[self.b_const])
            K.dma("sp", self.bfac[:], self.bfac_d[:, :], dcn, writes=[self.b_const])
            ds = self.dsem("c_stg")
            K.op("dve", lambda e: e.memset(self.ones[:], 1.0), writes=[self.b_const])
            stg = A("stg", [128, 128])
            stg_b = Buf("stg")
            for l in range(L):
                self.fm_load(self.gpre[:, l, :], w["g_mix_pre"][l].rearrange("(c p) -> c p", p=128), 8, stg, stg_b, ds)
                self.fm_load(self.gfpre[:, l, :], w["g_ffn_pre"][l].rearrange("(c p) -> c p", p=128), 8, stg, stg_b, ds)
                self.fm_load(self.gq[:, l, :], w["g_q"][l].rearrange("(c p) -> c p", p=128), 4, stg, stg_b, ds)
                self.fm_load(self.gkv[:, l, :], w["g_kv"][l].rearrange("(c p) -> c p", p=128), 2, stg, stg_b, ds)
                self.fm_load(self.bconv[:, l, :], w["b_conv"][l].rearrange("(c p) -> c p", p=128), NCH, stg, stg_b, ds)
                for k in range(3):
                    self.fm_load(self.wconv[:, l, k * NCH:(k + 1) * NCH],
                                 w["w_conv"][l, k].rearrange("(c p) -> c p", p=128), NCH, stg, stg_b, ds)
            cst = A("cst", [128, D])
            cst_b = Buf("cst")
            K.dma("sp", cst[0:NSEG, :], self.c_all[:, :], self.dsem("c_cst"), writes=[cst_b])
            sil = A("sil", [128, D])
            sil_b = Buf("sil")
            K.op("act", lambda e: e.activation(out=sil[0:NSEG, :], in_=cst[0:NSEG, :], func=AF.Silu),
                 reads=[cst_b], writes=[sil_b])
            sT = A("sT", [128, 8, NSEG], BF16)
            sT_b = Buf("sT")
            for kc in range(8):
                K.op("pe", lambda e: e.transpose(out=self.ps[:, 6, kc * NSEG:(kc + 1) * NSEG],
                                                 in_=sil[0:NSEG, kc * 128:(kc + 1) * 128],
                                                 identity=self.ident[0:NSEG, 0:NSEG]),
                     reads=[sil_b, self.b_const], writes=[self.pb[6]])
            K.op("dve", lambda e: e.tensor_copy(out=sT[:].rearrange("p a b -> p (a b)"), in_=self.ps[:, 6, 0:8 * NSEG]),
                 reads=[self.pb[6]], writes=[sT_b])
            modrow = A("modrow", [128, 6 * D])
            modrow_b = Buf("modrow")
            brow = A("brow", [128, 6 * D])
            brow_b = Buf("brow")
            dst = self.dsem("mst")
            for l in range(L):
                K.dma("sp", brow[0:NSEG, :], w["b_ada"][l].partition_broadcast(NSEG), self.dsem("c_brow"), writes=[brow_b])
                for cb in range(12):
                    wt, wb = self.wload(self.WADA[l, :, cb * 512:(cb + 1) * 512].rearrange("(k p) m -> p k m", p=128), (8, 512))
                    bk = cb % 4
                    for kc in range(8):
                        K.op("pe", lambda e: e.matmul(self.ps[0:NSEG, bk, :], lhsT=sT[:, kc, :], rhs=wt[:, kc, :],
                                                      start=(kc == 0), stop=(kc == 7)),
                             reads=[sT_b, wb], writes=[self.pb[bk]])
                    K.op("dve", lambda e: e.tensor_tensor(out=modrow[0:NSEG, cb * 512:(cb + 1) * 512], in0=self.ps[0:NSEG, bk, :],
                                                          in1=brow[0:NSEG, cb * 512:(cb + 1) * 512], op=ALU.add),
                         reads=[self.pb[bk], brow_b], writes=[modrow_b])
                K.dma(self.SQ, self.MODS[l], modrow[0:NSEG, :], dst, reads=[modrow_b])
                for c in range(48):
                    K.op("pe", lambda e: e.transpose(out=self.ps[:, 5, c * NSEG:(c + 1) * NSEG],
                                                     in_=modrow[0:NSEG, c * 128:(c + 1) * 128],
                                                     identity=self.ident[0:NSEG, 0:NSEG]),
                         reads=[modrow_b, self.b_const], writes=[self.pb[5]])
                K.op("dve", lambda e: e.tensor_copy(out=self.modF[:, l, :, :].rearrange("p a b -> p (a b)"),
                                                    in_=self.ps[:, 5, 0:48 * NSEG]),
                     reads=[self.pb[5]], writes=[self.b_vec])
                for s in range(NSEG):
                    K.op("dve", lambda e: e.scalar_tensor_tensor(out=self.A1[:, l, s, :], in0=self.modF[:, l, 8:16, s], scalar=1.0,
                                                                 in1=self.gpre[:, l, :], op0=ALU.add, op1=ALU.mult),
                         reads=[self.b_vec], writes=[self.b_vec])
                    K.op("dve", lambda e: e.scalar_tensor_tensor(out=self.A2[:, l, s, :], in0=self.modF[:, l, 32:40, s], scalar=1.0,
                                                                 in1=self.gfpre[:, l, :], op0=ALU.add, op1=ALU.mult),
                         reads=[self.b_vec], writes=[self.b_vec])
            K.barrier()

    def slot_layer(self, si, S, off, l):
        nc, K = self.nc, self.K
        xin = self.x_all if l == 0 else self.xa
        self.xout = self.y_all if l == self.L - 1 else self.xa
        stop = self.cfg.get("stop_after")
        with contextlib.ExitStack() as st:
            A = lambda n, s, d=F32: st.enter_context(nc.sbuf_tensor(self.nm(n), s, d))
            nseg = S // self.SP
            seg0 = off // self.SP
            G = A("G", [128, nseg, 2, D])
            G_b = Buf("G")
            ds = self.dsem("misc")
            with contextlib.ExitStack() as st2:
                gp = st2.enter_context(nc.sbuf_tensor(self.nm("gp"), [128, 2, D], F32))
                gp_b = Buf("gp")
                dgp = self.dsem("c_gp")
                ds = self.dsem("c_G")
                K.dma("sp", gp[:, 0, :], self.w["g_mix_post"][l].partition_broadcast(128), dgp, writes=[gp_b])
                K.dma("sp", gp[:, 1, :], self.w["g_ffn_post"][l].partition_broadcast(128), dgp, writes=[gp_b])
                for s in range(nseg):
                    K.dma("sp", G[:, s, 0, :], self.MODS[l, seg0 + s, 2 * D:3 * D].partition_broadcast(128), ds, writes=[G_b])
                    K.dma("sp", G[:, s, 1, :], self.MODS[l, seg0 + s, 5 * D:6 * D].partition_broadcast(128), ds, writes=[G_b])
                for s in range(nseg):
                    K.op("dve", lambda e: e.tensor_tensor(out=G[:, s, :, :], in0=G[:, s, :, :], in1=gp[:, :, :], op=ALU.mult),
                         reads=[gp_b, G_b], writes=[G_b])
                K.barrier()
            self.G, self.G_b = G, G_b
            self.ph1(S, off, l, xin)
            K.barrier()
            if stop == "ph1":
                return
            self.ph2(S, l)
            K.barrier()
            if stop == "ph2":
                return
            self.ph4(S, off, l, xin)
            K.barrier()
            if stop == "ph4":
                return
            if l not in self.cfg.get("skip5", ()):
                self.ph5(S, off, l)
            K.barrier()

    def rstd_from_ss(self, ss_ap, rs_ap, dim, ss_b, rs_b):
        K = self.K
        K.op("act", lambda e: e.activation(out=rs_ap, in_=ss_ap, func=AF.Sqrt, scale=1.0 / dim, bias=EPS),
             reads=[ss_b], writes=[rs_b])
        K.op("dve", lambda e: e.reciprocal(out=rs_ap, in_=rs_ap), reads=[rs_b], writes=[rs_b])

    def norm_T(self, xsrc, xsrc_b, hT, hT_b, col0, Asc, Bsc, scr, split_evac=False):
        K = self.K
        junk, junk_b, ss, ss_b, xn, xn_b = scr
        K.op("act", lambda e: e.activation(out=junk[:], in_=xsrc, func=AF.Square, accum_out=ss[:, 0:1]),
             reads=[xsrc_b], writes=[junk_b, ss_b])
        self.rstd_from_ss(ss[:, 0:1], ss[:, 1:2], D, ss_b, ss_b)
        K.op("dve", lambda e: e.tensor_scalar(out=xn[:], in0=xsrc, scalar1=ss[:, 1:2], scalar2=None, op0=ALU.mult),
             reads=[xsrc_b, ss_b], writes=[xn_b])
        for kc in range(8):
            K.op("pe", lambda e: e.transpose(out=self.ps[:, 6 + kc // 4, (kc % 4) * 128:(kc % 4 + 1) * 128],
                                             in_=xn[:, kc * 128:(kc + 1) * 128], identity=self.ident[:]),
                 reads=[xn_b, self.b_const], writes=[self.pb[6 + kc // 4]])
        for kc in range(8):
            src = self.ps[:, 6 + kc // 4, (kc % 4) * 128:(kc % 4 + 1) * 128]
            if split_evac and kc % 2 == 1:
                K.op("dve", lambda e: e.tensor_scalar(out=hT[:, kc, col0:col0 + 128], in0=src, scalar1=Asc(kc), scalar2=Bsc(kc),
                                                      op0=ALU.mult, op1=ALU.add),
                     reads=[self.pb[6 + kc // 4], self.b_vec], writes=[hT_b])
            else:
                K.op("act", lambda e: e.activation(out=hT[:, kc, col0:col0 + 128], in_=src,
                                                   func=AF.Identity, scale=Asc(kc), bias=Bsc(kc)),
                     reads=[self.pb[6 + kc // 4], self.b_vec], writes=[hT_b])

    def fm_rmsnorm(self, banks, nch, dim, gvec, outT, out_b, scr):
        K = self.K
        sq, sq_b, rs, rs_b, sumbank = scr
        for c in range(nch):
            K.op("act", lambda e: e.activation(out=sq[:, c, :], in_=self.ps[:, banks[c], :], func=AF.Square),
                 reads=[self.pb[banks[c]]], writes=[sq_b])
        for c in range(nch):
            K.op("pe", lambda e: e.matmul(self.ps[:, sumbank, :], lhsT=self.ones[:], rhs=sq[:, c, :],
                                          start=(c == 0), stop=(c == nch - 1)),
                 reads=[sq_b, self.b_const], writes=[self.pb[sumbank]])
        K.op("act", lambda e: e.activation(out=rs[:], in_=self.ps[:, sumbank, :], func=AF.Sqrt, scale=1.0 / dim, bias=EPS),
             reads=[self.pb[sumbank]], writes=[rs_b])
        K.op("dve", lambda e: e.reciprocal(out=rs[:], in_=rs[:]), reads=[rs_b], writes=[rs_b])
        for c in range(nch):
            K.op("dve", lambda e: e.scalar_tensor_tensor(out=outT[:, c, :], in0=self.ps[:, banks[c], :], scalar=gvec(c),
                                                         in1=rs[:], op0=ALU.mult, op1=ALU.mult),
                 reads=[self.pb[banks[c]], rs_b, self.b_vec], writes=[out_b])

    def ph1(self, S, off, l, xin):
        nc, K = self.nc, self.K
        ps, pb = self.ps, self.pb
        with contextlib.ExitStack() as st:
            A = lambda n, s, d=F32: st.enter_context(nc.sbuf_tensor(self.nm(n), s, d))
            self.wpool = Pool(K, st, "wst", self.NW, [128, 4096], BF16)
            xt = Pool(K, st, "p1x", 2, [128, 4, D], F32)
            tbp = Pool(K, st, "p1tb", 2, [128, 512], F32)
            junk = A("p1junk", [128, D]); junk_b = Buf()
            ssp = Pool(K, st, "p1ss", 2, [128, 2], F32, dma=False)
            xnp = Pool(K, st, "p1xn", 2, [128, D], F32, dma=False)
            hTp = Pool(K, st, "p1hT", 2, [128, 8, 512], BF16, dma=False)
            hTp.b = [MB() for _ in hTp.b]
            GT = A("p1GT", [128, 16, 512], BF16); GT_b = Buf()
            uT = A("p1uT", [128, 4, 512], BF16); uT_b = Buf()
            YZ = A("p1YZ", [128, 4, D], BF16); YZ_b = MB()
            csc = A("p1csc", [128, 256], BF16); csc_b = Buf()
            PK = A("p1PK", [128, 512]); PK_b = Buf()
            KKt = A("p1KK", [128, 512], BF16); KK_b = Buf()
            sq = A("p1sq", [128, 4, 512], BF16); sq_b = Buf()
            rs = A("p1rs", [128, 512]); rs_b = Buf()
            sq2 = A("p1sq2", [128, 2, 512], BF16); sq2_b = Buf()
            rs2 = A("p1rs2", [128, 512]); rs2_b = Buf()
            ckv = A("p1ckv", [128, 2, 512], BF16); ckv_b = Buf()
            cq = A("p1cq", [128, 4, 512], BF16); cq_b = Buf()
            KT = A("p1KT", [128, NH, 512], BF16); KT_b = Buf()
            Vt = A("p1Vt", [128, 4, D], BF16); Vt_b = MB()
            QT = A("p1QT", [128, NH, 2, 512], BF16); QT_b = MB()
            dst = [self.dsem(f"p1st{i}") for i in range(6)]
            K.dma("sp", csc[:], self.csc_d[S][:, :], self.dsem("p1csc"), writes=[csc_b])
            WINl = self.WIN[l]
            nt = S // 512
            rr = [0]

            def nb():
                b = rr[0]
                rr[0] = (b + 1) % 6
                return b

            def win_chunk(c0, width):
                return self.wload(WINl[:, c0:c0 + width].rearrange("(k p) m -> p k m", p=128), (8, width))

            def load_tile(t):
                t0 = off + t * 512
                xtile, xb_, xd = xt.next()
                K.dma("sp", xtile[:], xin[t0:t0 + 512, :].rearrange("(j p) d -> p j d", p=128), xd, writes=[xb_])
                tb, tb_b, tbd = tbp.next()
                K.dma("sp", tb[:], self.rope_t[:, t0:t0 + 512], tbd, writes=[tb_b])
                return xtile, xb_, tb, tb_b

            nxt = load_tile(0)
            for t in range(nt):
                t0 = off + t * 512
                ts = t * 512
                seg = t0 // self.SP
                xtile, xb_, tb, tb_b = nxt
                hT, hT_b, _ = hTp.next()

                def mm8(bank, wt, wb, mcol, mw=128):
                    for kc in range(8):
                        K.op("pe", lambda e: e.matmul(ps[0:mw, bank, :], lhsT=wt[:, kc, mcol:mcol + mw], rhs=hT[:, kc, :],
                                                      start=(kc == 0), stop=(kc == 7)),
                             reads=[wb, hT_b], writes=[pb[bank]])

                def gates(gb):
                    wt, wb = win_chunk(1408 + gb * 512, 512)
                    for m in range(4):
                        bank = nb()
                        mm8(bank, wt, wb, m * 128)
                        K.op("act", lambda e: e.activation(out=GT[:, gb * 4 + m, :], in_=ps[:, bank, :], func=AF.Sigmoid),
                             reads=[pb[bank]], writes=[GT_b])

                for j in range(4):
                    ss, ss_b, _ = ssp.next()
                    xn, xn_b, _ = xnp.next()
                    self.norm_T(xtile[:, j, :], xb_, hT, hT_b, j * 128,
                                lambda kc: self.A1[:, l, seg, kc:kc + 1], lambda kc: self.modF[:, l, 0 + kc, seg:seg + 1],
                                (junk, junk_b, ss, ss_b, xn, xn_b), split_evac=False)
                if t + 1 < nt:
                    nxt = load_tile(t + 1)
                wt, wb = win_chunk(512, 384)
                b0, b1, b2 = nb(), nb(), nb()
                mm8(b0, wt, wb, 0)
                mm8(b1, wt, wb, 128)
                mm8(b2, wt, wb, 256)
                K.op("dve", lambda e: e.tensor_tensor(out=PK[:], in0=ps[:, b2, :], in1=tb[:], op=ALU.mult),
                     reads=[pb[b2], tb_b], writes=[PK_b])
                b3 = nb()
                K.op("pe", lambda e: e.matmul(ps[:, b3, :], lhsT=self.sel[:], rhs=PK[:], start=True, stop=True),
                     reads=[PK_b, self.b_const], writes=[pb[b3]])
                K.op("dve", lambda e: e.tensor_copy(out=KKt[:], in_=ps[:, b3, :]), reads=[pb[b3]], writes=[KK_b])
                K.dma(self.SQ, self.KKS[:, ts:ts + 512], KKt[:], dst[2], reads=[KK_b])
                b4 = nb()
                self.fm_rmsnorm([b0, b1], 2, 256, lambda c: self.gkv[:, l, c:c + 1], ckv, ckv_b, (sq2, sq2_b, rs2, rs2_b, b4))
                gates(0)
                wt, wb = win_chunk(0, 512)
                qb = [nb(), nb(), nb(), nb()]
                for c in range(4):
                    mm8(qb[c], wt, wb, c * 128)
                b4 = nb()
                self.fm_rmsnorm(qb, 4, 512, lambda c: self.gq[:, l, c:c + 1], cq, cq_b, (sq, sq_b, rs, rs_b, b4))
                gates(1)
                gates(2)
                wk, wkb = self.wload(self.WK[l].rearrange("(k p) m -> p k m", p=128), (2, 1024))
                for h in range(NH):
                    bank = nb()
                    for kc in range(2):
                        K.op("pe", lambda e: e.matmul(ps[:, bank, :], lhsT=wk[:, kc, h * 128:(h + 1) * 128], rhs=ckv[:, kc, :],
                                                      start=(kc == 0), stop=(kc == 1)),
                             reads=[wkb, ckv_b], writes=[pb[bank]])
                    K.op("dve", lambda e: e.tensor_copy(out=KT[:, h, :], in_=ps[:, bank, :]), reads=[pb[bank]], writes=[KT_b])
                for q4 in range(2):
                    K.dma(self.SQ, self.KS[q4 * 4:q4 * 4 + 4, :, ts:ts + 512].rearrange("h p t -> p h t"), KT[:, q4 * 4:q4 * 4 + 4, :], dst[3], reads=[KT_b])
                wv, wvb = self.wload(self.WV[l].rearrange("(k p) m -> p k m", p=128), (2, 1024))
                for j in range(4):
                    for hf in range(2):
                        bank = nb()
                        for kc in range(2):
                            K.op("pe", lambda e: e.matmul(ps[:, bank, :], lhsT=ckv[:, kc, j * 128:(j + 1) * 128],
                                                          rhs=wv[:, kc, hf * 512:(hf + 1) * 512], start=(kc == 0), stop=(kc == 1)),
                                 reads=[wvb, ckv_b], writes=[pb[bank]])
                        eng = "act" if hf == 0 else "dve"
                        if eng == "act":
                            K.op("act", lambda e: e.activation(out=Vt[:, j, hf * 512:(hf + 1) * 512], in_=ps[:, bank, :], func=AF.Copy),
                                 reads=[pb[bank]], writes=[Vt_b])
                        else:
                            K.op("dve", lambda e: e.tensor_copy(out=Vt[:, j, hf * 512:(hf + 1) * 512], in_=ps[:, bank, :]),
                                 reads=[pb[bank]], writes=[Vt_b])
                K.dma(self.SQ, self.VS[ts:ts + 512, :].rearrange("(j p) d -> p j d", p=128), Vt[:], dst[4], reads=[Vt_b])
                gates(3)
                for q4 in range(4):
                    K.dma(self.SQ, self.GS[q4 * 4:q4 * 4 + 4, :, ts:ts + 512].rearrange("c p t -> p c t"), GT[:, q4 * 4:q4 * 4 + 4, :], dst[0], reads=[GT_b])
                for half in range(2):
                    wq, wqb = self.wload(self.WQ[l, :, half * 1024:(half + 1) * 1024].rearrange("(k p) m -> p k m", p=128), (4, 1024))
                    for hh in range(4):
                        h = half * 4 + hh
                        bank = nb()
                        for kc in range(4):
                            K.op("pe", lambda e: e.matmul(ps[:, bank, :], lhsT=wq[:, kc, hh * 256:hh * 256 + 128], rhs=cq[:, kc, :],
                                                          start=(kc == 0), stop=(kc == 3)),
                                 reads=[wqb, cq_b], writes=[pb[bank]])
                        K.op("act", lambda e: e.activation(out=QT[:, h, 0, :], in_=ps[:, bank, :], func=AF.Copy),
                             reads=[pb[bank]], writes=[QT_b])
                        bank = nb()
                        for kc in range(4):
                            K.op("pe", lambda e: e.matmul(ps[:, bank, :], lhsT=wq[:, kc, hh * 256 + 128:hh * 256 + 256], rhs=cq[:, kc, :],
                                                          start=(kc == 0), stop=(kc == 3)),
                                 reads=[wqb, cq_b], writes=[pb[bank]])
                        K.op("dve", lambda e: e.tensor_tensor(out=QT[:, h, 1, :], in0=ps[:, bank, :], in1=tb[:], op=ALU.mult),
                             reads=[pb[bank], tb_b], writes=[QT_b])
                for q4 in range(4):
                    K.dma(self.SQ, self.QS[q4 * 2:q4 * 2 + 2, :, ts:ts + 512].rearrange("h (a p) t -> p h a t", p=128), QT[:, q4 * 2:q4 * 2 + 2, :, :], dst[5], reads=[QT_b])
                wt, wb = win_chunk(896, 512)
                for g in range(4):
                    bank = nb()
                    mm8(bank, wt, wb, g * 128)
                    K.op("dve", lambda e: e.tensor_copy(out=uT[:, g, :], in_=ps[:, bank, :]), reads=[pb[bank]], writes=[uT_b])
                for j in range(4):
                    for g2 in range(2):
                        bank = nb()
                        for gg in range(2):
                            g = g2 * 2 + gg
                            K.op("pe", lambda e: e.matmul(ps[:, bank, gg * 256:(gg + 1) * 256], lhsT=uT[:, g, j * 128:(j + 1) * 128],
                                                          rhs=csc[:], start=True, stop=True),
                                 reads=[uT_b, csc_b], writes=[pb[bank]])
                        if g2 == 0:
                            K.op("act", lambda e: e.activation(out=YZ[:, j, g2 * 512:(g2 + 1) * 512], in_=ps[:, bank, :], func=AF.Copy),
                                 reads=[pb[bank]], writes=[YZ_b])
                        else:
                            K.op("dve", lambda e: e.tensor_copy(out=YZ[:, j, g2 * 512:(g2 + 1) * 512], in_=ps[:, bank, :]),
                                 reads=[pb[bank]], writes=[YZ_b])
                K.dma(self.SQ, self.YZS[ts:ts + 512, :].rearrange("(j p) d -> p j d", p=128), YZ[:], dst[1], reads=[YZ_b])

    def ph2(self, S, l):
        nc, K = self.nc, self.K
        ps, pb = self.ps, self.pb
        scale = 192.0 ** -0.5
        nkc = S // 128
        nqt = S // 512
        with contextlib.ExitStack() as st:
            A = lambda n, s, d=F32: st.enter_context(nc.sbuf_tensor(self.nm(n), s, d))
            V = A("p2V", [128, nkc, D], BF16); V_b = Buf()
            KK = A("p2KK", [128, S], BF16); KK_b = Buf()
            kt = Pool(K, st, "p2kt", 2, [128, S], BF16)
            qt_ = Pool(K, st, "p2q", 2, [128, 2, S], BF16)
            pT = Pool(K, st, "p2pT", 5, [128, 512], BF16, dma=False)
            rden = A("p2rd", [128, 512]); rden_b = Buf()
            ot = Pool(K, st, "p2ot", 2, [128, 512], BF16)
            dm = self.dsem("misc")
            for c4 in range(nkc // 4):
                K.dma("sp", V[:, c4 * 4:c4 * 4 + 4, :], self.VS[c4 * 512:(c4 + 1) * 512, :].rearrange("(c p) d -> p c d", p=128), self.dsem("p2V"), writes=[V_b])
            K.dma("sp", KK[:], self.KKS[:, 0:S], self.dsem("p2KK"), writes=[KK_b])
            sb = [0]
            ob = [0]
            def load_head(hh):
                ktile_, kb_, kd_ = kt.next()
                K.dma("sp", ktile_[:], self.KS[hh, :, 0:S], kd_, writes=[kb_])
                qtile_, qb_, qd_ = qt_.next()
                K.dma("sp", qtile_[:], self.QS[hh, :, 0:S].rearrange("(a p) t -> p a t", p=128), qd_, writes=[qb_])
                return ktile_, kb_, qtile_, qb_

            nxt_head = load_head(0)
            for h in range(NH):
                ktile, kb, qtile, qb = nxt_head
                if h + 1 < NH:
                    nxt_head = load_head(h + 1)
                for q in range(nqt):
                    qs = slice(q * 512, (q + 1) * 512)
                    obank = 4 + ob[0] * 2
                    dbank = obank + 1
                    ob[0] ^= 1

                    def scores(kc):
                        bank = sb[0]
                        sb[0] = (sb[0] + 1) % 4
                        K.op("pe", lambda e: e.matmul(ps[:, bank, :], lhsT=ktile[:, kc * 128:(kc + 1) * 128], rhs=qtile[:, 0, qs],
                                                      start=True, stop=False),
                             reads=[kb, qb], writes=[pb[bank]])
                        K.op("pe", lambda e: e.matmul(ps[:, bank, :], lhsT=KK[:, kc * 128:(kc + 1) * 128], rhs=qtile[:, 1, qs],
                                                      start=False, stop=True),
                             reads=[KK_b, qb], writes=[pb[bank]])
                        p, p_b, _ = pT.next()
                        bias_ = self.amask[:, kc, q:q + 1] if S > self.SP else 0.0
                        K.op("act", lambda e: e.activation(out=p[:], in_=ps[:, bank, :], func=AF.Exp, scale=scale, bias=bias_),
                             reads=[pb[bank], self.b_const], writes=[p_b])
                        return p, p_b

                    def pv(kc, p, p_b):
                        K.op("pe", lambda e: e.matmul(ps[:, obank, :], lhsT=V[:, kc, h * 128:(h + 1) * 128], rhs=p[:],
                                                      start=(kc == 0), stop=(kc == nkc - 1)),
                             reads=[V_b, p_b], writes=[pb[obank]])
                        K.op("pe", lambda e: e.matmul(ps[:, dbank, :], lhsT=self.ones[:], rhs=p[:],
                                                      start=(kc == 0), stop=(kc == nkc - 1)),
                             reads=[self.b_const, p_b], writes=[pb[dbank]])

                    LA = 3
                    pend = [scores(kk) for kk in range(min(LA, nkc))]
                    for kc in range(nkc):
                        p, p_b = pend.pop(0)
                        pv(kc, p, p_b)
                        if kc + LA < nkc:
                            pend.append(scores(kc + LA))
                    K.op("dve", lambda e: e.reciprocal(out=rden[:], in_=ps[:, dbank, :]), reads=[pb[dbank]], writes=[rden_b])
                    o, o_b, o_d = ot.next()
                    K.op("dve", lambda e: e.tensor_tensor(out=o[:], in0=ps[:, obank, :], in1=rden[:], op=ALU.mult),
                         reads=[pb[obank], rden_b], writes=[o_b])
                    K.dma(self.SQ, self.ATS[h, :, qs], o[:], o_d, reads=[o_b])

    def tm_epilogue(self, ybanks, xres, xres_b, Gt, out, out_b, scr):
        K = self.K
        ps, pb = self.ps, self.pb
        junk, junk_b, ss, ss_b, tmp, tmp_b = scr
        yv = ps[:, ybanks[0]:ybanks[0] + 2, :].rearrange("p a b -> p (a b)")
        ybs = [pb[ybanks[0]], pb[ybanks[1]]]
        K.op("act", lambda e: e.activation(out=junk[:], in_=yv, func=AF.Square, accum_out=ss[:, 0:1]),
             reads=ybs, writes=[junk_b, ss_b])
        self.rstd_from_ss(ss[:, 0:1], ss[:, 1:2], D, ss_b, ss_b)
        K.op("dve", lambda e: e.scalar_tensor_tensor(out=tmp[:], in0=yv, scalar=ss[:, 1:2], in1=Gt, op0=ALU.mult, op1=ALU.mult),
             reads=ybs + [ss_b, self.G_b], writes=[tmp_b])
        K.op("pool", lambda e: e.tensor_tensor(out=out, in0=tmp[:], in1=xres, op=ALU.add),
             reads=[tmp_b, xres_b], writes=[out_b])

    def ph4(self, S, off, l, xin):
        nc, K = self.nc, self.K
        ps, pb = self.ps, self.pb
        nkc = S // 128
        with contextlib.ExitStack() as st:
            A = lambda n, s, d=F32: st.enter_context(nc.sbuf_tensor(self.nm(n), s, d))
            xt = Pool(K, st, "p4x", 1, [128, 4, D], F32)
            xo = Pool(K, st, "p4xo", 2, [128, D], F32)
            yz = Pool(K, st, "p4yz", 2, [128, 4, D], BF16)
            dcs = Pool(K, st, "p4dc", 2, [128, 2, 4, 512], BF16)
            h2p = Pool(K, st, "p4h2", 2, [128, 8, 512], BF16)
            AT = A("p4AT", [128, 8, 512], BF16); AT_b = Buf(); AT_d = self.dsem("p4AT")
            GT = A("p4GT", [128, 16, 512], BF16); GT_b = Buf(); GT_d = self.dsem("p4GT")
            FT = A("p4FT", [128, 4, 512], BF16); FT_b = Buf()
            mT = A("p4mT", [128, 8, 512], BF16); mT_b = Buf()
            t1 = A("p4t1", [128, 512]); t1_b = Buf()
            t2 = A("p4t2", [128, 512]); t2_b = Buf()
            junk = A("p4junk", [128, D]); junk_b = Buf()
            ss = A("p4ss", [128, 2]); ss_b = Buf()
            tmp = A("p4tmp", [128, D]); tmp_b = Buf()
            ss2 = A("p4ss2", [128, 2]); ss2_b = Buf()
            xn = A("p4xn", [128, D]); xn_b = Buf()
            dfc, dfs = self.dft_c[S], self.dft_s[S]
            wa = A("p4wa", [128, 8, D], BF16); wab = Buf()
            wo = A("p4wo", [128, 8, D], BF16); wob = Buf()
            wf = A("p4wf", [128, 4, D], BF16); wfb = Buf()
            for hk in range(2):
                K.dma("sp", wa[:, hk * 4:hk * 4 + 4, :], self.WAO[l, hk * 512:(hk + 1) * 512, :].rearrange("(k p) m -> p k m", p=128),
                      self.dsem("p4wa"), writes=[wab])
                K.dma("sp", wo[:, hk * 4:hk * 4 + 4, :], self.WO[l, hk * 512:(hk + 1) * 512, :].rearrange("(k p) m -> p k m", p=128),
                      self.dsem("p4wo"), writes=[wob])
            K.dma("sp", wf[:], self.WF[l].rearrange("(k p) m -> p k m", p=128), self.dsem("p4wf"), writes=[wfb])
            for t in range(S // 512):
                t0 = off + t * 512
                ts = t * 512
                seg = t0 // self.SP
                sloc = ts // self.SP
                xtile, xb_, xd = xt.next()
                K.dma("sp", xtile[:], xin[t0:t0 + 512, :].rearrange("(j p) d -> p j d", p=128), xd, writes=[xb_])
                for q4 in range(2):
                    K.dma("sp", AT[:, q4 * 4:q4 * 4 + 4, :], self.ATS[q4 * 4:q4 * 4 + 4, :, ts:ts + 512].rearrange("h p t -> p h t"), AT_d, writes=[AT_b])
                for q4 in range(4):
                    K.dma("sp", GT[:, q4 * 4:q4 * 4 + 4, :], self.GS[q4 * 4:q4 * 4 + 4, :, ts:ts + 512].rearrange("c p t -> p c t"), GT_d, writes=[GT_b])
                for n4 in range(nkc // 4):
                    yzt, yzb, yzd = yz.next()
                    K.dma("sp", yzt[:], self.YZS[n4 * 512:(n4 + 1) * 512, :].rearrange("(c p) d -> p c d", p=128), yzd, writes=[yzb])
                    dct, dcb, dcd = dcs.next()
                    K.dma("sp", dct[:, 0, :, :], dfc[n4 * 512:(n4 + 1) * 512, ts:ts + 512].rearrange("(c p) k -> p c k", p=128), dcd, writes=[dcb])
                    K.dma("sp", dct[:, 1, :, :], dfs[n4 * 512:(n4 + 1) * 512, ts:ts + 512].rearrange("(c p) k -> p c k", p=128), dcd, writes=[dcb])
                    for g in range(4):
                        for c in range(4):
                            n = n4 * 4 + c
                            for cs in range(2):
                                K.op("pe", lambda e: e.matmul(ps[:, g, :], lhsT=yzt[:, c, g * 256 + cs * 128:g * 256 + (cs + 1) * 128],
                                                              rhs=dct[:, cs, c, :], start=(n == 0 and cs == 0),
                                                              stop=(n == nkc - 1 and cs == 1)),
                                     reads=[yzb, dcb], writes=[pb[g]])
                for g in range(4):
                    K.op("act", lambda e: e.activation(out=FT[:, g, :], in_=ps[:, g, :], func=AF.Copy), reads=[pb[g]], writes=[FT_b])
                for hf in range(2):
                    for mm in range(4):
                        m = hf * 4 + mm
                        ba = 4 + (m % 2) * 2
                        bb = ba + 1
                        for kc in range(8):
                            K.op("pe", lambda e: e.matmul(ps[:, ba, :], lhsT=wa[:, kc, m * 128:(m + 1) * 128], rhs=AT[:, kc, :],
                                                          start=(kc == 0), stop=(kc == 7)),
                                 reads=[wab, AT_b], writes=[pb[ba]])
                        for kc in range(4):
                            K.op("pe", lambda e: e.matmul(ps[:, bb, :], lhsT=wf[:, kc, m * 128:(m + 1) * 128], rhs=FT[:, kc, :],
                                                          start=(kc == 0), stop=(kc == 3)),
                                 reads=[wfb, FT_b], writes=[pb[bb]])
                        K.op("dve", lambda e: e.tensor_tensor(out=t1[:], in0=ps[:, ba, :], in1=GT[:, m, :], op=ALU.mult),
                             reads=[pb[ba], GT_b], writes=[t1_b])
                        K.op("dve", lambda e: e.tensor_tensor(out=t2[:], in0=ps[:, bb, :], in1=GT[:, 8 + m, :], op=ALU.mult),
                             reads=[pb[bb], GT_b], writes=[t2_b])
                        K.op("pool", lambda e: e.tensor_tensor(out=mT[:, m, :], in0=t1[:], in1=t2[:], op=ALU.add),
                             reads=[t1_b, t2_b], writes=[mT_b])
                h2t, h2b, h2d = h2p.next()
                for jp in range(2):
                    for hf in range(2):
                        for jj in range(2):
                            j = jp * 2 + jj
                            bank = jj * 2 + hf
                            for kc in range(8):
                                K.op("pe", lambda e: e.matmul(ps[:, bank, :], lhsT=mT[:, kc, j * 128:(j + 1) * 128], rhs=wo[:, kc, hf * 512:(hf + 1) * 512],
                                                              start=(kc == 0), stop=(kc == 7)),
                                     reads=[wob, mT_b], writes=[pb[bank]])
                    for jj in range(2):
                        j = jp * 2 + jj
                        xot, xob, xod = xo.next()
                        self.tm_epilogue([jj * 2, jj * 2 + 1], xtile[:, j, :], xb_, self.G[:, sloc, 0, :], xot[:], xob,
                                         (junk, junk_b, ss, ss_b, tmp, tmp_b))
                        K.dma(self.SQ, self.xb[t0 + j * 128:t0 + (j + 1) * 128, :], xot[:], xod, reads=[xob])
                        self.norm_T(xot[:], xob, h2t, h2b, j * 128,
                                    lambda kc: self.A2[:, l, seg, kc:kc + 1], lambda kc: self.modF[:, l, 24 + kc, seg:seg + 1],
                                    (junk, junk_b, ss2, ss2_b, xn, xn_b))
                for q4 in range(2):
                    K.dma(self.SQ, self.H2S[q4 * 4:q4 * 4 + 4, :, 1 + ts:1 + ts + 512].rearrange("c p t -> p c t"), h2t[:, q4 * 4:q4 * 4 + 4, :], h2d, reads=[h2b])

    def ph5(self, S, off, l):
        nc, K = self.nc, self.K
        ps, pb = self.ps, self.pb
        with contextlib.ExitStack() as st:
            A = lambda n, s, d=F32: st.enter_context(nc.sbuf_tensor(self.nm(n), s, d))
            self.wpool = Pool(K, st, "wst", self.NW, [128, 4096], BF16)
            xt = Pool(K, st, "p5x", 2, [128, 4, D], F32)
            xo = Pool(K, st, "p5xo", 2, [128, D], F32)
            h2p = Pool(K, st, "p5h2", 2, [128, 8, 514], BF16)
            halo = A("p5halo", [128, NCH, 2]); halo_b = Buf()
            up = Pool(K, st, "p5up", 3, [128, 2, 514], F32, dma=False)
            cv = Pool(K, st, "p5cv", 3, [128, 2, 512], F32, dma=False)
            glp = Pool(K, st, "p5gl", 2, [128, 512], F32, dma=False)
            up.b = [(MB(), MB()) for _ in up.b]
            cv.b = [(Buf(), Buf()) for _ in cv.b]
            gT = A("p5gT", [128, 22, 512], BF16); gT_b = Buf()
            junk = A("p5junk", [128, D]); junk_b = Buf()
            ss = A("p5ss", [128, 2]); ss_b = Buf()
            tmp = A("p5tmp", [128, D]); tmp_b = Buf()
            WUPl, WDNl = self.WUP[l], self.WDN[l]
            hbufs = [Buf() for _ in range(4)]
            upr = [0]
            wc = lambda k, c: self.wconv[:, l, k * NCH + c:k * NCH + c + 1]
            def load_tile(t):
                t0 = off + t * 512
                ts = t * 512
                xtile, xb_, xd = xt.next()
                K.dma("sp", xtile[:], self.xb[t0:t0 + 512, :].rearrange("(j p) d -> p j d", p=128), xd, writes=[xb_])
                h2T, h2T_b, h2d = h2p.next()
                c_lo = 1 if t == 0 else 0
                c_hi = 513 if t == S // 512 - 1 else 514
                for hh_ in range(2):
                    K.dma("sp", h2T[:, hh_ * 4:hh_ * 4 + 4, c_lo:c_hi],
                          self.H2S[hh_ * 4:hh_ * 4 + 4, :, ts + c_lo:ts + c_hi].rearrange("c p t -> p c t"), h2d, writes=[h2T_b])
                if t == 0:
                    K.op("dve", lambda e: e.memset(h2T[:, :, 0:1], 0.0), writes=[h2T_b])
                if t == S // 512 - 1:
                    K.op("dve", lambda e: e.memset(h2T[:, :, 513:514], 0.0), writes=[h2T_b])
                return xtile, xb_, h2T, h2T_b

            nxt = load_tile(0)
            pend_g = []

            def flush_gelu():
                while pend_g:
                    pend_g.pop(0)()

            for t in range(S // 512):
                t0 = off + t * 512
                ts = t * 512
                sloc = ts // self.SP
                xtile, xb_, h2T, h2T_b = nxt
                hmain = lambda kc: h2T[:, kc, 1:513]
                hhalo = lambda kc: h2T[:, kc, 0:514:513]
                groups = [(g * 4, 4) for g in range(5)] + [(20, 2)]
                for (j0, nj) in groups:
                    wa, wab = self.wload(WUPl[:, j0 * 128:(j0 + nj) * 128].rearrange("(k p) m -> p k m", p=128), (8, nj * 128))
                    wb2, wbb = self.wload(WUPl[:, DFF + j0 * 128:DFF + (j0 + nj) * 128].rearrange("(k p) m -> p k m", p=128), (8, nj * 128))
                    hb = 4
                    for ab, (wt_, wtb) in enumerate(((wa, wab), (wb2, wbb))):
                        for jj in range(nj):
                            ch = j0 + jj + ab * 22
                            for kc in range(8):
                                K.op("pe", lambda e: e.matmul(ps[:, hb, ch * 2:ch * 2 + 2], lhsT=wt_[:, kc, jj * 128:(jj + 1) * 128], rhs=hhalo(kc),
                                                              start=(kc == 0), stop=(kc == 7)),
                                     reads=[wtb, h2T_b], writes=[pb[hb]])
                    for ab in range(2):
                        c0 = (j0 + ab * 22) * 2
                        K.op("act", lambda e: e.activation(out=halo[:, j0 + ab * 22:j0 + ab * 22 + nj, :].rearrange("p a b -> p (a b)"),
                                                           in_=ps[:, hb, c0:c0 + 2 * nj], func=AF.Copy),
                             reads=[pb[hb]], writes=[halo_b])
                        for col, edge in ((0, ts), (1, ts + 512)):
                            if edge % self.SP == 0 and 0 < edge < S:
                                hv = halo[:, j0 + ab * 22:j0 + ab * 22 + nj, col:col + 1]
                                K.op("dve", lambda e: e.tensor_scalar(out=hv, in0=hv, scalar1=self.bfac[:, 0:1], scalar2=None, op0=ALU.mult),
                                     reads=[halo_b, self.b_const], writes=[halo_b])
                    for jj in range(nj):
                        j = j0 + jj
                        upt, upb2, _ = up.next()
                        cvt, cvb2, _ = cv.next()
                        for ab, (wt_, wtb) in enumerate(((wa, wab), (wb2, wbb))):
                            upb, cvb = upb2[ab], cvb2[ab]
                            ch = j + ab * 22
                            bank = 5 + upr[0]
                            upr[0] = (upr[0] + 1) % 3
                            for kc in range(8):
                                K.op("pe", lambda e: e.matmul(ps[:, bank, :], lhsT=wt_[:, kc, jj * 128:(jj + 1) * 128], rhs=hmain(kc),
                                                              start=(kc == 0), stop=(kc == 7)),
                                     reads=[wtb, h2T_b], writes=[pb[bank]])
                            K.op("act", lambda e: e.activation(out=upt[:, ab, 1:513], in_=ps[:, bank, :], func=AF.Copy),
                                 reads=[pb[bank]], writes=[upb])
                            K.op("dve", lambda e: e.tensor_copy(out=upt[:, ab, 0:514:513], in_=halo[:, ch, :]),
                                 reads=[halo_b], writes=[upb])
                            if ab == 0:
                                K.op("act", lambda e: e.activation(out=cvt[:, ab, :], in_=ps[:, bank, :], func=AF.Identity,
                                                                   scale=wc(1, ch), bias=self.bconv[:, l, ch:ch + 1]),
                                     reads=[pb[bank], self.b_vec], writes=[cvb])
                                flush_gelu()
                            else:
                                K.op("pool", lambda e: e.tensor_scalar(out=cvt[:, ab, :], in0=upt[:, ab, 1:513], scalar1=wc(1, ch),
                                                                       scalar2=self.bconv[:, l, ch:ch + 1], op0=ALU.mult, op1=ALU.add),
                                     reads=[upb, self.b_vec], writes=[cvb])
                            K.op("dve", lambda e: e.scalar_tensor_tensor(out=cvt[:, ab, :], in0=upt[:, ab, 0:512], scalar=wc(0, ch),
                                                                         in1=cvt[:, ab, :], op0=ALU.mult, op1=ALU.add),
                                 reads=[upb, cvb, self.b_vec], writes=[cvb])
                            K.op("dve", lambda e: e.scalar_tensor_tensor(out=cvt[:, ab, :], in0=upt[:, ab, 2:514], scalar=wc(2, ch),
                                                                         in1=cvt[:, ab, :], op0=ALU.mult, op1=ALU.add),
                                 reads=[upb, cvb, self.b_vec], writes=[cvb])
                        def gelu_(cvt=cvt, cvb2=cvb2, j=j):
                            gl, gl_b, _ = glp.next()
                            K.op("act", lambda e: e.activation(out=gl[:], in_=cvt[:, 0, :], func=AF.Gelu_apprx_tanh),
                                 reads=[cvb2[0]], writes=[gl_b])
                            K.op("pool", lambda e: e.tensor_tensor(out=gT[:, j, :], in0=gl[:], in1=cvt[:, 1, :], op=ALU.mult),
                                 reads=[gl_b, cvb2[1]], writes=[gT_b])
                        pend_g.append(gelu_)
                flush_gelu()
                if t + 1 < S // 512:
                    nxt = load_tile(t + 1)
                kgroups = [(g * 4, 4) for g in range(5)] + [(20, 2)]
                for jp in range(2):
                    for (k0, nk) in kgroups:
                        wd, wdb = self.wload(WDNl[k0 * 128:(k0 + nk) * 128, :].rearrange("(k p) m -> p k m", p=128), (nk, 1024))
                        for jj in range(2):
                            j = jp * 2 + jj
                            for hf in range(2):
                                bank = jj * 2 + hf
                                for kk in range(nk):
                                    kc = k0 + kk
                                    K.op("pe", lambda e: e.matmul(ps[:, bank, :], lhsT=gT[:, kc, j * 128:(j + 1) * 128],
                                                                  rhs=wd[:, kk, hf * 512:(hf + 1) * 512], start=(kc == 0), stop=(kc == 21)),
                                         reads=[wdb, gT_b], writes=[pb[bank]])
                    for jj in range(2):
                        j = jp * 2 + jj
                        xot, xob, xod = xo.next()
                        self.tm_epilogue([jj * 2, jj * 2 + 1], xtile[:, j, :], xb_, self.G[:, sloc, 1, :], xot[:], xob,
                                         (junk, junk_b, ss, ss_b, tmp, tmp_b))
                        K.dma(self.SQ, self.xout[t0 + j * 128:t0 + (j + 1) * 128, :], xot[:], xod, reads=[xob])


def host_consts(slots, seg, split_big):
    c = {}
    c["ident"] = np.eye(128, dtype=np.float32)
    dd = np.arange(128)
    c["sel"] = (dd[:, None] % 64 == dd[None, :] % 64).astype(np.float32)
    half = 32
    inv = (1.0 / (10000.0 ** (np.arange(half, dtype=np.float32) / np.float32(half)))).astype(np.float32)

    def rope_cols(S):
        ang = (np.arange(S, dtype=np.float32)[:, None] * inv[None, :]).astype(np.float32)
        cs, sn = np.cos(ang).T.astype(np.float32), np.sin(ang).T.astype(np.float32)
        return np.concatenate([cs, cs, -sn, sn], axis=0)

    cols = []
    for si, S in enumerate(slots):
        if si == 0 and split_big:
            cols += [rope_cols(seg)] * (S // seg)
        else:
            cols.append(rope_cols(S))
    c["rope_t"] = np.ascontiguousarray(np.concatenate(cols, axis=1).astype(np.float32))
    SB = max(slots)
    am = np.zeros((128, SB // 128, SB // 512), np.float32)
    if split_big:
        kseg = (np.arange(SB // 128) * 128) // seg
        qseg = (np.arange(SB // 512) * 512) // seg
        am[:, kseg[:, None] != qseg[None, :]] = -30000.0
    c["amask"] = am
    c["bfac"] = np.full((128, 1), 0.0 if split_big else 1.0, np.float32)
    ch = np.arange(128, dtype=np.float64)
    ach = 2.0 * np.pi * ((ch[:, None] * ch[None, :]) % 128) / 128.0

    def dft(S):
        n = np.arange(S, dtype=np.int64)
        a = (2.0 * np.pi / S) * ((n[:, None] * n[None, :]) % S).astype(np.float64)
        return np.cos(a), -np.sin(a)

    for S in sorted(set(slots)):
        eff = seg if (S == slots[0] and split_big) else S
        nrm = 1.0 / np.sqrt(128.0 * eff)
        c[f"csc{S}"] = np.concatenate([np.cos(ach) * nrm, np.sin(ach) * nrm], axis=1).astype(ml_dtypes.bfloat16)
        if eff == S:
            dc, dsn = dft(S)
        else:
            bc, bs = dft(eff)
            dc = np.zeros((S, S)); dsn = np.zeros((S, S))
            for b in range(S // eff):
                dc[b * eff:(b + 1) * eff, b * eff:(b + 1) * eff] = bc
                dsn[b * eff:(b + 1) * eff, b * eff:(b + 1) * eff] = bs
        c[f"dftc{S}"] = dc.astype(ml_dtypes.bfloat16)
        c[f"dfts{S}"] = dsn.astype(ml_dtypes.bfloat16)
    return c


WNAMES = ["w_ada", "b_ada", "g_mix_pre", "g_mix_post", "w_in", "g_q", "w_q_b", "g_kv", "w_kv_b", "w_attn_o",
          "w_four", "w_out", "g_ffn_pre", "g_ffn_post", "w_up", "w_conv", "b_conv", "w_down"]

_CACHE = {}


def get_prog(cfg_key, cfg):
    if cfg_key not in _CACHE:
        _CACHE[cfg_key] = Prog(cfg)
    return _CACHE[cfg_key]


def kernel(x_prompt, x_sample, c_prompt, c_sample, **weights):
    x_prompt = np.asarray(x_prompt, np.float32)
    x_sample = np.asarray(x_sample, np.float32)
    c_prompt = np.asarray(c_prompt, np.float32)
    c_sample = np.asarray(c_sample, np.float32)
    L = int(np.asarray(weights["w_in"]).shape[0])
    SB, SPL = x_sample.shape[1], x_prompt.shape[1]
    nsmp, nprm = x_sample.shape[0], x_prompt.shape[0]
    nbig = SB // SPL
    nB = N_CORES - nsmp
    npp = (nprm - nB * nbig) // N_CORES
    assert nsmp * npp + nB * (npp + nbig) == nprm and SB % SPL == 0
    slots = [SB] + [SPL] * npp
    cfg = {"depth": L, "slots": slots, "seg": SPL}
    prog = get_prog(("full", L, tuple(slots)), cfg)
    cA = host_consts(slots, SPL, False)
    cB = host_consts(slots, SPL, True)
    wts = {k: np.ascontiguousarray(np.asarray(weights[k], np.float32)) for k in WNAMES}
    in_maps, owner = [], []
    nxt = 0
    for core in range(N_CORES):
        if core < nsmp:
            xs = [x_sample[core]]
            cs = [c_sample[core]] * nbig
            own = [("s", core)]
        else:
            ids = list(range(nxt, nxt + nbig))
            nxt += nbig
            xs = [np.concatenate([x_prompt[i] for i in ids], axis=0)]
            cs = [c_prompt[i] for i in ids]
            own = [("pb", ids)]
        ids = list(range(nxt, nxt + npp))
        nxt += npp
        xs += [x_prompt[i] for i in ids]
        cs += [c_prompt[i] for i in ids]
        own += [("p", i) for i in ids]
        m = {"x_all": np.ascontiguousarray(np.concatenate(xs, axis=0)),
             "c_all": np.ascontiguousarray(np.stack(cs, axis=0))}
        m.update(cA if core < nsmp else cB)
        m.update(wts)
        in_maps.append(m)
        owner.append(own)
    res = run_bass_kernel_spmd(prog.nc, in_maps, core_ids=list(range(N_CORES)))
    y_prompt = np.empty_like(x_prompt)
    y_sample = np.empty_like(x_sample)
    for core in range(N_CORES):
        y = np.asarray(res.results[core]["y_all"], np.float32)
        off = 0
        for kind, idx in owner[core]:
            if kind == "s":
                y_sample[idx] = y[off:off + SB]
                off += SB
            elif kind == "pb":
                for i in idx:
                    y_prompt[i] = y[off:off + SPL]
                    off += SPL
            else:
                y_prompt[idx] = y[off:off + SPL]
                off += SPL
    return (y_prompt, y_sample)
```

```python
import contextlib
import numpy as np
import ml_dtypes
import concourse.bass as bass
import concourse.mybir as mybir
from concourse.bass_utils import run_bass_kernel_spmd

F32 = mybir.dt.float32
BF16 = mybir.dt.bfloat16
AF = mybir.ActivationFunctionType
ALU = mybir.AluOpType

D = 1024
NH = 8
DFF = 2816
NCH = 44
WIN_W = 3456
EPS = 1e-6
N_CORES = 8


class TL:
    __slots__ = ("sem", "count", "name")

    def __init__(self, sem, name):
        self.sem = sem
        self.count = 0
        self.name = name


class Buf:
    __slots__ = ("w", "r", "name", "x")

    def __init__(self, name="", x=False):
        self.w = None
        self.r = {}
        self.name = name
        self.x = x


class MB:
    def __init__(self):
        self.d = {}

    def w(self, en):
        if en not in self.d:
            self.d[en] = Buf(en)
        return [self.d[en]]

    def all(self):
        return list(self.d.values())


class Eng:
    def __init__(self, name, h, sem):
        self.name = name
        self.h = h
        self.tl = TL(sem, name)
        self.seen = {}


class KB:
    def __init__(self, nc):
        self.nc = nc
        self.E = {}
        for name, h in [("pe", nc.tensor), ("act", nc.scalar), ("dve", nc.vector),
                        ("pool", nc.gpsimd), ("sp", nc.sync)]:
            self.E[name] = Eng(name, h, nc.alloc_semaphore("s_" + name))
        self.dsems = []
        self.nins = 0

    def new_dsem(self, name):
        tl = TL(self.nc.alloc_semaphore(name), name)
        self.dsems.append(tl)
        return tl

    @staticmethod
    def _deps(reads, writes, own=None):
        deps = {}
        for b in reads:
            if b.w is not None:
                tl, v = b.w
                if deps.get(tl, 0) < v:
                    deps[tl] = v
            if b.x:
                for tl, v in b.r.items():
                    if tl is not own and deps.get(tl, 0) < v:
                        deps[tl] = v
        for b in writes:
            if b.w is not None:
                tl, v = b.w
                if tl is not own and deps.get(tl, 0) < v:
                    deps[tl] = v
            for tl, v in b.r.items():
                if deps.get(tl, 0) < v:
                    deps[tl] = v
        return deps

    def _wait(self, e, deps):
        for tl, v in deps.items():
            if tl is e.tl and e.name == "pe":
                continue
            if e.seen.get(tl, 0) < v:
                e.h.wait_ge(tl.sem, v)
                e.seen[tl] = v

    @staticmethod
    def _exp(lst, en):
        out = []
        for b in lst:
            if isinstance(b, MB):
                out.extend(b.all() if en is None else b.w(en))
            else:
                out.append(b)
        return out

    def op(self, en, fn, reads=(), writes=()):
        reads = self._exp(reads, None)
        writes = self._exp(writes, en)
        e = self.E[en]
        self._wait(e, self._deps(reads, writes, e.tl))
        ins = fn(e.h)
        e.tl.count += 1
        ins.then_inc(e.tl.sem, 1)
        v = e.tl.count
        for b in reads:
            b.r[e.tl] = v
        for b in writes:
            b.w = (e.tl, v)
            b.r = {}
        self.nins += 1
        return ins

    def dma(self, q, out, in_, dsem, reads=(), writes=(), **kw):
        reads = self._exp(reads, None)
        writes = self._exp(writes, "dma")
        e = self.E[q]
        self._wait(e, self._deps(reads, writes, dsem))
        ins = e.h.dma_start(out=out, in_=in_, **kw)
        ins.then_inc(dsem.sem, 16)
        dsem.count += 16
        v = dsem.count
        for b in reads:
            b.r[dsem] = v
        for b in writes:
            b.w = (dsem, v)
            b.r = {}
        self.nins += 1
        return ins

    def barrier(self):
        tls = [e.tl for e in self.E.values()] + self.dsems
        for e in self.E.values():
            for tl in tls:
                if tl is e.tl and e.name in ("pe", "sp"):
                    continue
                if tl.count > e.seen.get(tl, 0):
                    e.h.wait_ge(tl.sem, tl.count)
                    e.seen[tl] = tl.count


class Pool:
    def __init__(self, K, st, name, n, shape, dtype, dma=True):
        self.t = [st.enter_context(K.nc.sbuf_tensor(K.nm(f"{name}{i}"), shape, dtype)) for i in range(n)]
        self.b = [Buf(f"{name}{i}") for i in range(n)]
        self.d = [K.dsem(f"{name}{i}") for i in range(n)] if dma else None
        self.i = 0
        self.n = n

    def next(self):
        i = self.i
        self.i = (i + 1) % self.n
        return self.t[i], self.b[i], (self.d[i] if self.d else None)


class Prog:
    def __init__(self, cfg):
        self.cfg = cfg
        self.L = cfg["depth"]
        self.slots = cfg["slots"]
        self.SP = cfg["seg"]
        self.SMAX = max(self.slots)
        self.NTOK = sum(self.slots)
        self.NSEG = self.NTOK // self.SP
        self.dbg = cfg.get("dbg")
        self.nc = bass.Bass("TRN2", target_bir_lowering=False)
        self.K = KB(self.nc)
        self._dsem_cache = {}
        self.K.dsem = self.dsem
        self._uid = 0
        self.SQ = cfg.get("store_q", "pool")
        self.K.nm = self.nm
        self.build()

    def dsem(self, name):
        if name not in self._dsem_cache:
            self._dsem_cache[name] = self.K.new_dsem("d_" + name)
        return self._dsem_cache[name]

    def nm(self, n):
        self._uid += 1
        return f"{n}_{self._uid}"

    def din(self, name, shape, dt=F32):
        return self.nc.dram_tensor(name, list(shape), dt, kind="ExternalInput").ap()

    def dscr(self, name, shape, dt=BF16):
        kind = "ExternalOutput" if (self.dbg and name in self.dbg) else "Internal"
        return self.nc.dram_tensor(name, list(shape), dt, kind=kind).ap()

    def build(self):
        nc, K, L = self.nc, self.K, self.L
        NTOK, NSEG, SMAX = self.NTOK, self.NSEG, self.SMAX
        self.x_all = self.din("x_all", [NTOK, D])
        self.c_all = self.din("c_all", [NSEG, D])
        self.rope_t = self.din("rope_t", [128, NTOK])
        self.ident_d = self.din("ident", [128, 128])
        self.sel_d = self.din("sel", [128, 128])
        self.amask_d = self.din("amask", [128, SMAX // 128, SMAX // 512])
        self.bfac_d = self.din("bfac", [128, 1])
        self.csc_d = {}
        self.dft_c = {}
        self.dft_s = {}
        for S in sorted(set(self.slots)):
            self.csc_d[S] = self.din(f"csc{S}", [128, 256], BF16)
            self.dft_c[S] = self.din(f"dftc{S}", [S, S], BF16)
            self.dft_s[S] = self.din(f"dfts{S}", [S, S], BF16)
        w = {}
        w["w_ada"] = self.din("w_ada", [L, D, 6 * D])
        w["b_ada"] = self.din("b_ada", [L, 6 * D])
        w["g_mix_pre"] = self.din("g_mix_pre", [L, D])
        w["g_mix_post"] = self.din("g_mix_post", [L, D])
        w["w_in"] = self.din("w_in", [L, D, 3392])
        w["g_q"] = self.din("g_q", [L, 512])
        w["w_q_b"] = self.din("w_q_b", [L, 512, 1536])
        w["g_kv"] = self.din("g_kv", [L, 256])
        w["w_kv_b"] = self.din("w_kv_b", [L, 256, 2048])
        w["w_attn_o"] = self.din("w_attn_o", [L, D, D])
        w["w_four"] = self.din("w_four", [L, 512, D])
        w["w_out"] = self.din("w_out", [L, D, D])
        w["g_ffn_pre"] = self.din("g_ffn_pre", [L, D])
        w["g_ffn_post"] = self.din("g_ffn_post", [L, D])
        w["w_up"] = self.din("w_up", [L, D, 2 * DFF])
        w["w_conv"] = self.din("w_conv", [L, 3, 2 * DFF])
        w["b_conv"] = self.din("b_conv", [L, 2 * DFF])
        w["w_down"] = self.din("w_down", [L, DFF, D])
        self.w = w
        self.y_all = self.nc.dram_tensor("y_all", [NTOK, D], F32, kind="ExternalOutput").ap()
        self.xb = self.dscr("xb", [NTOK, D], F32)
        self.xa = self.dscr("xa", [NTOK, D], F32)
        self.WADA = self.dscr("WADA", [L, D, 6 * D])
        self.WIN = self.dscr("WIN", [L, D, WIN_W])
        self.WQ = self.dscr("WQ", [L, 512, 2048])
        self.WK = self.dscr("WK", [L, 256, 1024])
        self.WV = self.dscr("WV", [L, 256, 1024])
        self.WAO = self.dscr("WAO", [L, D, D])
        self.WF = self.dscr("WF", [L, 512, D])
        self.WO = self.dscr("WO", [L, D, D])
        self.WUP = self.dscr("WUP", [L, D, 2 * DFF])
        self.WDN = self.dscr("WDN", [L, DFF, D])
        self.MODS = self.dscr("MODS", [L, NSEG, 6 * D], F32)
        self.QS = self.dscr("QS", [NH, 256, SMAX])
        self.KS = self.dscr("KS", [NH, 128, SMAX])
        self.KKS = self.dscr("KKS", [128, SMAX])
        self.VS = self.dscr("VS", [SMAX, D])
        self.GS = self.dscr("GS", [16, 128, SMAX])
        self.YZS = self.dscr("YZS", [SMAX, D])
        self.ATS = self.dscr("ATS", [NH, 128, SMAX])
        self.H2S = self.dscr("H2S", [8, 128, SMAX + 2])

        with contextlib.ExitStack() as st:
            self.perm(st)
            self.prepass()
            K.barrier()
            self.prologue()
            K.barrier()
            off = 0
            for si, S in enumerate(self.slots):
                for l in range(L):
                    self.slot_layer(si, S, off, l)
                off += S
            K.barrier()

    def perm(self, st):
        nc, L, NSEG = self.nc, self.L, self.NSEG
        A = lambda n, s, d=F32: st.enter_context(nc.sbuf_tensor(self.nm(n), s, d))
        self.ident = A("ident_s", [128, 128])
        self.sel = A("sel_s", [128, 128])
        self.ones = A("ones_s", [128, 128], BF16)
        self.amask = A("amask_s", [128, self.SMAX // 128, self.SMAX // 512])
        self.bfac = A("bfac_s", [128, 1])
        self.b_const = Buf("const")
        self.gpre = A("gpre", [128, L, 8])
        self.gfpre = A("gfpre", [128, L, 8])
        self.gq = A("gq", [128, L, 4])
        self.gkv = A("gkv", [128, L, 2])
        self.wconv = A("wconv", [128, L, 3 * NCH])
        self.bconv = A("bconv", [128, L, NCH])
        self.modF = A("modF", [128, L, 48, NSEG])
        self.A1 = A("A1", [128, L, NSEG, 8])
        self.A2 = A("A2", [128, L, NSEG, 8])
        self.b_vec = Buf("vec")
        self.ps = st.enter_context(nc.psum_tensor("ps", [128, 8, 512], F32))
        self.pb = [Buf(f"ps{i}", x=True) for i in range(8)]
        self.NW = 4

    def prepass(self):
        K, w, L = self.K, self.w, self.L
        ds = self.dsem("pre")
        b = Buf("prew")
        self.b_wscr = b

        ncp = [0]

        def cp(dst, src):
            K.dma("pool", dst, src, ds, writes=[b])
            ncp[0] += 1
            if ncp[0] % 2 == 0:
                e = K.E["pool"]
                e.h.wait_ge(ds.sem, ds.count)
                e.seen[ds] = ds.count

        for l in range(L):
            for r in range(0, D, 256):
                cp(self.WADA[l, r:r + 256, :], w["w_ada"][l, r:r + 256, :])
                cp(self.WUP[l, r:r + 256, :], w["w_up"][l, r:r + 256, :])
                cp(self.WIN[l, r:r + 256, 0:832], w["w_in"][l, r:r + 256, 0:832])
                cp(self.WIN[l, r:r + 256, 832:864], w["w_in"][l, r:r + 256, 800:832])
                cp(self.WIN[l, r:r + 256, 864:896], w["w_in"][l, r:r + 256, 768:800])
                cp(self.WIN[l, r:r + 256, 896:WIN_W], w["w_in"][l, r:r + 256, 832:3392])
                cp(self.WAO[l, r:r + 256, :], w["w_attn_o"][l, r:r + 256, :])
                cp(self.WO[l, r:r + 256, :], w["w_out"][l, r:r + 256, :])
            for r in range(0, DFF, 256):
                cp(self.WDN[l, r:r + 256, :], w["w_down"][l, r:r + 256, :])
            cp(self.WF[l], w["w_four"][l])
            wq_d = self.WQ[l].rearrange("k (h c) -> k h c", h=NH)
            wq_s = w["w_q_b"][l].rearrange("k (h c) -> k h c", h=NH)
            wkv = w["w_kv_b"][l].rearrange("k (h c) -> k h c", h=NH)
            wk_d = self.WK[l].rearrange("k (h c) -> k h c", h=NH)
            wv_d = self.WV[l].rearrange("k (h c) -> k h c", h=NH)
            for h in range(NH):
                cp(wq_d[:, h, 0:192], wq_s[:, h, 0:192])
                cp(wq_d[:, h, 192:224], wq_s[:, h, 160:192])
                cp(wq_d[:, h, 224:256], wq_s[:, h, 128:160])
                cp(wk_d[:, h, :], wkv[:, h, 0:128])
                cp(wv_d[:, h, :], wkv[:, h, 128:256])

    def wload(self, src_ap, view):
        t, b, d = self.wpool.next()
        n = 1
        for s in view:
            n *= s
        dst = t[:, 0:n]
        if len(view) == 2:
            dst = dst.rearrange("p (a b) -> p a b", a=view[0])
        if len(view) == 2 and view[0] * 128 > 512:
            hk = view[0] // 2
            self.K.dma("sp", dst[:, 0:hk, :], src_ap[:, 0:hk, :], d, writes=[b])
            self.K.dma("sp", dst[:, hk:, :], src_ap[:, hk:, :], d, writes=[b])
        else:
            self.K.dma("sp", dst, src_ap, d, writes=[b])
        return dst, b

    def fm_load(self, dst, src2d, n, stg, stg_b, ds):
        K = self.K
        K.dma("sp", stg[0:n, :], src2d, ds, writes=[stg_b])
        K.op("pe", lambda e: e.transpose(out=self.ps[:, 7, 0:n], in_=stg[0:n, :], identity=self.ident[0:n, 0:n]),
             reads=[stg_b, self.b_const], writes=[self.pb[7]])
        K.op("dve", lambda e: e.tensor_copy(out=dst, in_=self.ps[:, 7, 0:n]), reads=[self.pb[7]], writes=[self.b_vec])

    def prologue(self):
        nc, K, L, NSEG, w = self.nc, self.K, self.L, self.NSEG, self.w
        ds = self.dsem("misc")
        with contextlib.ExitStack() as st:
            A = lambda n, s, d=F32: st.enter_context(nc.sbuf_tensor(self.nm(n), s, d))
            self.wpool = Pool(K, st, "wst", self.NW, [128, 4096], BF16)
            dcn = self.dsem("c_id")
            K.dma("sp", self.ident[:], self.ident_d[:, :], dcn, writes=[self.b_const])
            K.dma("sp", self.sel[:], self.sel_d[:, :], dcn, writes=[self.b_const])
            K.dma("sp", self.amask[:], self.amask_d[:, :, :], dcn, writes=[self.b_const])
            K.dma("sp", self.bfac[:], self.bfac_d[:, :], dcn, writes=[self.b_const])
            ds = self.dsem("c_stg")
            K.op("dve", lambda e: e.memset(self.ones[:], 1.0), writes=[self.b_const])
            stg = A("stg", [128, 128])
            stg_b = Buf("stg")
            for l in range(L):
                self.fm_load(self.gpre[:, l, :], w["g_mix_pre"][l].rearrange("(c p) -> c p", p=128), 8, stg, stg_b, ds)
                self.fm_load(self.gfpre[:, l, :], w["g_ffn_pre"][l].rearrange("(c p) -> c p", p=128), 8, stg, stg_b, ds)
                self.fm_load(self.gq[:, l, :], w["g_q"][l].rearrange("(c p) -> c p", p=128), 4, stg, stg_b, ds)
                self.fm_load(self.gkv[:, l, :], w["g_kv"][l].rearrange("(c p) -> c p", p=128), 2, stg, stg_b, ds)
                self.fm_load(self.bconv[:, l, :], w["b_conv"][l].rearrange("(c p) -> c p", p=128), NCH, stg, stg_b, ds)
                for k in range(3):
                    self.fm_load(self.wconv[:, l, k * NCH:(k + 1) * NCH],
                                 w["w_conv"][l, k].rearrange("(c p) -> c p", p=128), NCH, stg, stg_b, ds)
            cst = A("cst", [128, D])
            cst_b = Buf("cst")
            K.dma("sp", cst[0:NSEG, :], self.c_all[:, :], self.dsem("c_cst"), writes=[cst_b])
            sil = A("sil", [128, D])
            sil_b = Buf("sil")
            K.op("act", lambda e: e.activation(out=sil[0:NSEG, :], in_=cst[0:NSEG, :], func=AF.Silu),
                 reads=[cst_b], writes=[sil_b])
            sT = A("sT", [128, 8, NSEG], BF16)
            sT_b = Buf("sT")
            for kc in range(8):
                K.op("pe", lambda e: e.transpose(out=self.ps[:, 6, kc * NSEG:(kc + 1) * NSEG],
                                                 in_=sil[0:NSEG, kc * 128:(kc + 1) * 128],
                                                 identity=self.ident[0:NSEG, 0:NSEG]),
                     reads=[sil_b, self.b_const], writes=[self.pb[6]])
            K.op("dve", lambda e: e.tensor_copy(out=sT[:].rearrange("p a b -> p (a b)"), in_=self.ps[:, 6, 0:8 * NSEG]),
                 reads=[self.pb[6]], writes=[sT_b])
            modrow = A("modrow", [128, 6 * D])
            modrow_b = Buf("modrow")
            brow = A("brow", [128, 6 * D])
            brow_b = Buf("brow")
            dst = self.dsem("mst")
            for l in range(L):
                K.dma("sp", brow[0:NSEG, :], w["b_ada"][l].partition_broadcast(NSEG), self.dsem("c_brow"), writes=[brow_b])
                for cb in range(12):
                    wt, wb = self.wload(self.WADA[l, :, cb * 512:(cb + 1) * 512].rearrange("(k p) m -> p k m", p=128), (8, 512))
                    bk = cb % 4
                    for kc in range(8):
                        K.op("pe", lambda e: e.matmul(self.ps[0:NSEG, bk, :], lhsT=sT[:, kc, :], rhs=wt[:, kc, :],
                                                      start=(kc == 0), stop=(kc == 7)),
                             reads=[sT_b, wb], writes=[self.pb[bk]])
                    K.op("dve", lambda e: e.tensor_tensor(out=modrow[0:NSEG, cb * 512:(cb + 1) * 512], in0=self.ps[0:NSEG, bk, :],
                                                          in1=brow[0:NSEG, cb * 512:(cb + 1) * 512], op=ALU.add),
                         reads=[self.pb[bk], brow_b], writes=[modrow_b])
                K.dma(self.SQ, self.MODS[l], modrow[0:NSEG, :], dst, reads=[modrow_b])
                for c in range(48):
                    K.op("pe", lambda e: e.transpose(out=self.ps[:, 5, c * NSEG:(c + 1) * NSEG],
                                                     in_=modrow[0:NSEG, c * 128:(c + 1) * 128],
                                                     identity=self.ident[0:NSEG, 0:NSEG]),
                         reads=[modrow_b, self.b_const], writes=[self.pb[5]])
                K.op("dve", lambda e: e.tensor_copy(out=self.modF[:, l, :, :].rearrange("p a b -> p (a b)"),
                                                    in_=self.ps[:, 5, 0:48 * NSEG]),
                     reads=[self.pb[5]], writes=[self.b_vec])
                for s in range(NSEG):
                    K.op("dve", lambda e: e.scalar_tensor_tensor(out=self.A1[:, l, s, :], in0=self.modF[:, l, 8:16, s], scalar=1.0,
                                                                 in1=self.gpre[:, l, :], op0=ALU.add, op1=ALU.mult),
                         reads=[self.b_vec], writes=[self.b_vec])
                    K.op("dve", lambda e: e.scalar_tensor_tensor(out=self.A2[:, l, s, :], in0=self.modF[:, l, 32:40, s], scalar=1.0,
                                                                 in1=self.gfpre[:, l, :], op0=ALU.add, op1=ALU.mult),
                         reads=[self.b_vec], writes=[self.b_vec])
            K.barrier()

    def slot_layer(self, si, S, off, l):
        nc, K = self.nc, self.K
        xin = self.x_all if l == 0 else self.xa
        self.xout = self.y_all if l == self.L - 1 else self.xa
        stop = self.cfg.get("stop_after")
        with contextlib.ExitStack() as st:
            A = lambda n, s, d=F32: st.enter_context(nc.sbuf_tensor(self.nm(n), s, d))
            nseg = S // self.SP
            seg0 = off // self.SP
            G = A("G", [128, nseg, 2, D])
            G_b = Buf("G")
            ds = self.dsem("misc")
            with contextlib.ExitStack() as st2:
                gp = st2.enter_context(nc.sbuf_tensor(self.nm("gp"), [128, 2, D], F32))
                gp_b = Buf("gp")
                dgp = self.dsem("c_gp")
                ds = self.dsem("c_G")
                K.dma("sp", gp[:, 0, :], self.w["g_mix_post"][l].partition_broadcast(128), dgp, writes=[gp_b])
                K.dma("sp", gp[:, 1, :], self.w["g_ffn_post"][l].partition_broadcast(128), dgp, writes=[gp_b])
                for s in range(nseg):
                    K.dma("sp", G[:, s, 0, :], self.MODS[l, seg0 + s, 2 * D:3 * D].partition_broadcast(128), ds, writes=[G_b])
                    K.dma("sp", G[:, s, 1, :], self.MODS[l, seg0 + s, 5 * D:6 * D].partition_broadcast(128), ds, writes=[G_b])
                for s in range(nseg):
                    K.op("dve", lambda e: e.tensor_tensor(out=G[:, s, :, :], in0=G[:, s, :, :], in1=gp[:, :, :], op=ALU.mult),
                         reads=[gp_b, G_b], writes=[G_b])
                K.barrier()
            self.G, self.G_b = G, G_b
            self.ph1(S, off, l, xin)
            K.barrier()
            if stop == "ph1":
                return
            self.ph2(S, l)
            K.barrier()
            if stop == "ph2":
                return
            self.ph4(S, off, l, xin)
            K.barrier()
            if stop == "ph4":
                return
            if l not in self.cfg.get("skip5", ()):
                self.ph5(S, off, l)
            K.barrier()

    def rstd_from_ss(self, ss_ap, rs_ap, dim, ss_b, rs_b):
        K = self.K
        K.op("act", lambda e: e.activation(out=rs_ap, in_=ss_ap, func=AF.Sqrt, scale=1.0 / dim, bias=EPS),
             reads=[ss_b], writes=[rs_b])
        K.op("dve", lambda e: e.reciprocal(out=rs_ap, in_=rs_ap), reads=[rs_b], writes=[rs_b])

    def norm_T(self, xsrc, xsrc_b, hT, hT_b, col0, Asc, Bsc, scr, split_evac=False):
        K = self.K
        junk, junk_b, ss, ss_b, xn, xn_b = scr
        K.op("act", lambda e: e.activation(out=junk[:], in_=xsrc, func=AF.Square, accum_out=ss[:, 0:1]),
             reads=[xsrc_b], writes=[junk_b, ss_b])
        self.rstd_from_ss(ss[:, 0:1], ss[:, 1:2], D, ss_b, ss_b)
        K.op("dve", lambda e: e.tensor_scalar(out=xn[:], in0=xsrc, scalar1=ss[:, 1:2], scalar2=None, op0=ALU.mult),
             reads=[xsrc_b, ss_b], writes=[xn_b])
        for kc in range(8):
            K.op("pe", lambda e: e.transpose(out=self.ps[:, 6 + kc // 4, (kc % 4) * 128:(kc % 4 + 1) * 128],
                                             in_=xn[:, kc * 128:(kc + 1) * 128], identity=self.ident[:]),
                 reads=[xn_b, self.b_const], writes=[self.pb[6 + kc // 4]])
        for kc in range(8):
            src = self.ps[:, 6 + kc // 4, (kc % 4) * 128:(kc % 4 + 1) * 128]
            if split_evac and kc >= 4:
                K.op("dve", lambda e: e.tensor_scalar(out=hT[:, kc, col0:col0 + 128], in0=src, scalar1=Asc(kc), scalar2=Bsc(kc),
                                                      op0=ALU.mult, op1=ALU.add),
                     reads=[self.pb[6 + kc // 4], self.b_vec], writes=[hT_b])
            else:
                K.op("act", lambda e: e.activation(out=hT[:, kc, col0:col0 + 128], in_=src,
                                                   func=AF.Identity, scale=Asc(kc), bias=Bsc(kc)),
                     reads=[self.pb[6 + kc // 4], self.b_vec], writes=[hT_b])

    def fm_rmsnorm(self, banks, nch, dim, gvec, outT, out_b, scr):
        K = self.K
        sq, sq_b, rs, rs_b, sumbank = scr
        for c in range(nch):
            K.op("act", lambda e: e.activation(out=sq[:, c, :], in_=self.ps[:, banks[c], :], func=AF.Square),
                 reads=[self.pb[banks[c]]], writes=[sq_b])
        for c in range(nch):
            K.op("pe", lambda e: e.matmul(self.ps[:, sumbank, :], lhsT=self.ones[:], rhs=sq[:, c, :],
                                          start=(c == 0), stop=(c == nch - 1)),
                 reads=[sq_b, self.b_const], writes=[self.pb[sumbank]])
        K.op("act", lambda e: e.activation(out=rs[:], in_=self.ps[:, sumbank, :], func=AF.Sqrt, scale=1.0 / dim, bias=EPS),
             reads=[self.pb[sumbank]], writes=[rs_b])
        K.op("dve", lambda e: e.reciprocal(out=rs[:], in_=rs[:]), reads=[rs_b], writes=[rs_b])
        for c in range(nch):
            K.op("dve", lambda e: e.scalar_tensor_tensor(out=outT[:, c, :], in0=self.ps[:, banks[c], :], scalar=gvec(c),
                                                         in1=rs[:], op0=ALU.mult, op1=ALU.mult),
                 reads=[self.pb[banks[c]], rs_b, self.b_vec], writes=[out_b])

    def ph1(self, S, off, l, xin):
        nc, K = self.nc, self.K
        ps, pb = self.ps, self.pb
        with contextlib.ExitStack() as st:
            A = lambda n, s, d=F32: st.enter_context(nc.sbuf_tensor(self.nm(n), s, d))
            self.wpool = Pool(K, st, "wst", self.NW, [128, 4096], BF16)
            xt = Pool(K, st, "p1x", 2, [128, 4, D], F32)
            tbp = Pool(K, st, "p1tb", 2, [128, 512], F32)
            junk = A("p1junk", [128, D]); junk_b = Buf()
            ssp = Pool(K, st, "p1ss", 2, [128, 2], F32, dma=False)
            xnp = Pool(K, st, "p1xn", 2, [128, D], F32, dma=False)
            hTp = Pool(K, st, "p1hT", 2, [128, 8, 512], BF16, dma=False)
            hTp.b = [MB() for _ in hTp.b]
            GT = A("p1GT", [128, 16, 512], BF16); GT_b = Buf()
            uT = A("p1uT", [128, 4, 512], BF16); uT_b = Buf()
            YZ = A("p1YZ", [128, 4, D], BF16); YZ_b = MB()
            csc = A("p1csc", [128, 256], BF16); csc_b = Buf()
            PK = A("p1PK", [128, 512]); PK_b = Buf()
            KKt = A("p1KK", [128, 512], BF16); KK_b = Buf()
            sq = A("p1sq", [128, 4, 512], BF16); sq_b = Buf()
            rs = A("p1rs", [128, 512]); rs_b = Buf()
            sq2 = A("p1sq2", [128, 2, 512], BF16); sq2_b = Buf()
            rs2 = A("p1rs2", [128, 512]); rs2_b = Buf()
            ckv = A("p1ckv", [128, 2, 512], BF16); ckv_b = Buf()
            cq = A("p1cq", [128, 4, 512], BF16); cq_b = Buf()
            KT = A("p1KT", [128, NH, 512], BF16); KT_b = Buf()
            Vt = A("p1Vt", [128, 4, D], BF16); Vt_b = MB()
            QT = A("p1QT", [128, NH, 2, 512], BF16); QT_b = MB()
            dst = [self.dsem(f"p1st{i}") for i in range(6)]
            K.dma("sp", csc[:], self.csc_d[S][:, :], self.dsem("p1csc"), writes=[csc_b])
            WINl = self.WIN[l]
            nt = S // 512
            rr = [0]

            def nb():
                b = rr[0]
                rr[0] = (b + 1) % 6
                return b

            def win_chunk(c0, width):
                return self.wload(WINl[:, c0:c0 + width].rearrange("(k p) m -> p k m", p=128), (8, width))

            def load_tile(t):
                t0 = off + t * 512
                xtile, xb_, xd = xt.next()
                K.dma("sp", xtile[:], xin[t0:t0 + 512, :].rearrange("(j p) d -> p j d", p=128), xd, writes=[xb_])
                tb, tb_b, tbd = tbp.next()
                K.dma("sp", tb[:], self.rope_t[:, t0:t0 + 512], tbd, writes=[tb_b])
                return xtile, xb_, tb, tb_b

            nxt = load_tile(0)
            for t in range(nt):
                t0 = off + t * 512
                ts = t * 512
                seg = t0 // self.SP
                xtile, xb_, tb, tb_b = nxt
                hT, hT_b, _ = hTp.next()

                def mm8(bank, wt, wb, mcol, mw=128):
                    for kc in range(8):
                        K.op("pe", lambda e: e.matmul(ps[0:mw, bank, :], lhsT=wt[:, kc, mcol:mcol + mw], rhs=hT[:, kc, :],
                                                      start=(kc == 0), stop=(kc == 7)),
                             reads=[wb, hT_b], writes=[pb[bank]])

                def gates(gb):
                    wt, wb = win_chunk(1408 + gb * 512, 512)
                    for m in range(4):
                        bank = nb()
                        mm8(bank, wt, wb, m * 128)
                        K.op("act", lambda e: e.activation(out=GT[:, gb * 4 + m, :], in_=ps[:, bank, :], func=AF.Sigmoid),
                             reads=[pb[bank]], writes=[GT_b])

                for j in range(4):
                    ss, ss_b, _ = ssp.next()
                    xn, xn_b, _ = xnp.next()
                    self.norm_T(xtile[:, j, :], xb_, hT, hT_b, j * 128,
                                lambda kc: self.A1[:, l, seg, kc:kc + 1], lambda kc: self.modF[:, l, 0 + kc, seg:seg + 1],
                                (junk, junk_b, ss, ss_b, xn, xn_b), split_evac=True)
                if t + 1 < nt:
                    nxt = load_tile(t + 1)
                wt, wb = win_chunk(512, 384)
                b0, b1, b2 = nb(), nb(), nb()
                mm8(b0, wt, wb, 0)
                mm8(b1, wt, wb, 128)
                mm8(b2, wt, wb, 256)
                K.op("dve", lambda e: e.tensor_tensor(out=PK[:], in0=ps[:, b2, :], in1=tb[:], op=ALU.mult),
                     reads=[pb[b2], tb_b], writes=[PK_b])
                b3 = nb()
                K.op("pe", lambda e: e.matmul(ps[:, b3, :], lhsT=self.sel[:], rhs=PK[:], start=True, stop=True),
                     reads=[PK_b, self.b_const], writes=[pb[b3]])
                K.op("dve", lambda e: e.tensor_copy(out=KKt[:], in_=ps[:, b3, :]), reads=[pb[b3]], writes=[KK_b])
                K.dma(self.SQ, self.KKS[:, ts:ts + 512], KKt[:], dst[2], reads=[KK_b])
                b4 = nb()
                self.fm_rmsnorm([b0, b1], 2, 256, lambda c: self.gkv[:, l, c:c + 1], ckv, ckv_b, (sq2, sq2_b, rs2, rs2_b, b4))
                gates(0)
                wt, wb = win_chunk(0, 512)
                qb = [nb(), nb(), nb(), nb()]
                for c in range(4):
                    mm8(qb[c], wt, wb, c * 128)
                b4 = nb()
                self.fm_rmsnorm(qb, 4, 512, lambda c: self.gq[:, l, c:c + 1], cq, cq_b, (sq, sq_b, rs, rs_b, b4))
                gates(1)
                gates(2)
                wk, wkb = self.wload(self.WK[l].rearrange("(k p) m -> p k m", p=128), (2, 1024))
                for h in range(NH):
                    bank = nb()
                    for kc in range(2):
                        K.op("pe", lambda e: e.matmul(ps[:, bank, :], lhsT=wk[:, kc, h * 128:(h + 1) * 128], rhs=ckv[:, kc, :],
                                                      start=(kc == 0), stop=(kc == 1)),
                             reads=[wkb, ckv_b], writes=[pb[bank]])
                    K.op("dve", lambda e: e.tensor_copy(out=KT[:, h, :], in_=ps[:, bank, :]), reads=[pb[bank]], writes=[KT_b])
                for q4 in range(2):
                    K.dma(self.SQ, self.KS[q4 * 4:q4 * 4 + 4, :, ts:ts + 512].rearrange("h p t -> p h t"), KT[:, q4 * 4:q4 * 4 + 4, :], dst[3], reads=[KT_b])
                wv, wvb = self.wload(self.WV[l].rearrange("(k p) m -> p k m", p=128), (2, 1024))
                for j in range(4):
                    for hf in range(2):
                        bank = nb()
                        for kc in range(2):
                            K.op("pe", lambda e: e.matmul(ps[:, bank, :], lhsT=ckv[:, kc, j * 128:(j + 1) * 128],
                                                          rhs=wv[:, kc, hf * 512:(hf + 1) * 512], start=(kc == 0), stop=(kc == 1)),
                                 reads=[wvb, ckv_b], writes=[pb[bank]])
                        eng = "act" if hf == 0 else "dve"
                        if eng == "act":
                            K.op("act", lambda e: e.activation(out=Vt[:, j, hf * 512:(hf + 1) * 512], in_=ps[:, bank, :], func=AF.Copy),
                                 reads=[pb[bank]], writes=[Vt_b])
                        else:
                            K.op("dve", lambda e: e.tensor_copy(out=Vt[:, j, hf * 512:(hf + 1) * 512], in_=ps[:, bank, :]),
                                 reads=[pb[bank]], writes=[Vt_b])
                K.dma(self.SQ, self.VS[ts:ts + 512, :].rearrange("(j p) d -> p j d", p=128), Vt[:], dst[4], reads=[Vt_b])
                gates(3)
                for q4 in range(4):
                    K.dma(self.SQ, self.GS[q4 * 4:q4 * 4 + 4, :, ts:ts + 512].rearrange("c p t -> p c t"), GT[:, q4 * 4:q4 * 4 + 4, :], dst[0], reads=[GT_b])
                for half in range(2):
                    wq, wqb = self.wload(self.WQ[l, :, half * 1024:(half + 1) * 1024].rearrange("(k p) m -> p k m", p=128), (4, 1024))
                    for hh in range(4):
                        h = half * 4 + hh
                        bank = nb()
                        for kc in range(4):
                            K.op("pe", lambda e: e.matmul(ps[:, bank, :], lhsT=wq[:, kc, hh * 256:hh * 256 + 128], rhs=cq[:, kc, :],
                                                          start=(kc == 0), stop=(kc == 3)),
                                 reads=[wqb, cq_b], writes=[pb[bank]])
                        K.op("act", lambda e: e.activation(out=QT[:, h, 0, :], in_=ps[:, bank, :], func=AF.Copy),
                             reads=[pb[bank]], writes=[QT_b])
                        bank = nb()
                        for kc in range(4):
                            K.op("pe", lambda e: e.matmul(ps[:, bank, :], lhsT=wq[:, kc, hh * 256 + 128:hh * 256 + 256], rhs=cq[:, kc, :],
                                                          start=(kc == 0), stop=(kc == 3)),
                                 reads=[wqb, cq_b], writes=[pb[bank]])
                        K.op("dve", lambda e: e.tensor_tensor(out=QT[:, h, 1, :], in0=ps[:, bank, :], in1=tb[:], op=ALU.mult),
                             reads=[pb[bank], tb_b], writes=[QT_b])
                for q4 in range(4):
                    K.dma(self.SQ, self.QS[q4 * 2:q4 * 2 + 2, :, ts:ts + 512].rearrange("h (a p) t -> p h a t", p=128), QT[:, q4 * 2:q4 * 2 + 2, :, :], dst[5], reads=[QT_b])
                wt, wb = win_chunk(896, 512)
                for g in range(4):
                    bank = nb()
                    mm8(bank, wt, wb, g * 128)
                    K.op("dve", lambda e: e.tensor_copy(out=uT[:, g, :], in_=ps[:, bank, :]), reads=[pb[bank]], writes=[uT_b])
                for j in range(4):
                    for g2 in range(2):
                        bank = nb()
                        for gg in range(2):
                            g = g2 * 2 + gg
                            K.op("pe", lambda e: e.matmul(ps[:, bank, gg * 256:(gg + 1) * 256], lhsT=uT[:, g, j * 128:(j + 1) * 128],
                                                          rhs=csc[:], start=True, stop=True),
                                 reads=[uT_b, csc_b], writes=[pb[bank]])
                        if g2 == 0:
                            K.op("act", lambda e: e.activation(out=YZ[:, j, g2 * 512:(g2 + 1) * 512], in_=ps[:, bank, :], func=AF.Copy),
                                 reads=[pb[bank]], writes=[YZ_b])
                        else:
                            K.op("dve", lambda e: e.tensor_copy(out=YZ[:, j, g2 * 512:(g2 + 1) * 512], in_=ps[:, bank, :]),
                                 reads=[pb[bank]], writes=[YZ_b])
                K.dma(self.SQ, self.YZS[ts:ts + 512, :].rearrange("(j p) d -> p j d", p=128), YZ[:], dst[1], reads=[YZ_b])

    def ph2(self, S, l):
        nc, K = self.nc, self.K
        ps, pb = self.ps, self.pb
        scale = 192.0 ** -0.5
        nkc = S // 128
        nqt = S // 512
        with contextlib.ExitStack() as st:
            A = lambda n, s, d=F32: st.enter_context(nc.sbuf_tensor(self.nm(n), s, d))
            V = A("p2V", [128, nkc, D], BF16); V_b = Buf()
            KK = A("p2KK", [128, S], BF16); KK_b = Buf()
            kt = Pool(K, st, "p2kt", 2, [128, S], BF16)
            qt_ = Pool(K, st, "p2q", 2, [128, 2, S], BF16)
            pT = Pool(K, st, "p2pT", 5, [128, 512], BF16, dma=False)
            rden = A("p2rd", [128, 512]); rden_b = Buf()
            ot = Pool(K, st, "p2ot", 2, [128, 512], BF16)
            dm = self.dsem("misc")
            for c4 in range(nkc // 4):
                K.dma("sp", V[:, c4 * 4:c4 * 4 + 4, :], self.VS[c4 * 512:(c4 + 1) * 512, :].rearrange("(c p) d -> p c d", p=128), self.dsem("p2V"), writes=[V_b])
            K.dma("sp", KK[:], self.KKS[:, 0:S], self.dsem("p2KK"), writes=[KK_b])
            sb = [0]
            ob = [0]
            def load_head(hh):
                ktile_, kb_, kd_ = kt.next()
                K.dma("sp", ktile_[:], self.KS[hh, :, 0:S], kd_, writes=[kb_])
                qtile_, qb_, qd_ = qt_.next()
                K.dma("sp", qtile_[:], self.QS[hh, :, 0:S].rearrange("(a p) t -> p a t", p=128), qd_, writes=[qb_])
                return ktile_, kb_, qtile_, qb_

            nxt_head = load_head(0)
            for h in range(NH):
                ktile, kb, qtile, qb = nxt_head
                if h + 1 < NH:
                    nxt_head = load_head(h + 1)
                for q in range(nqt):
                    qs = slice(q * 512, (q + 1) * 512)
                    obank = 4 + ob[0] * 2
                    dbank = obank + 1
                    ob[0] ^= 1

                    def scores(kc):
                        bank = sb[0]
                        sb[0] = (sb[0] + 1) % 4
                        K.op("pe", lambda e: e.matmul(ps[:, bank, :], lhsT=ktile[:, kc * 128:(kc + 1) * 128], rhs=qtile[:, 0, qs],
                                                      start=True, stop=False),
                             reads=[kb, qb], writes=[pb[bank]])
                        K.op("pe", lambda e: e.matmul(ps[:, bank, :], lhsT=KK[:, kc * 128:(kc + 1) * 128], rhs=qtile[:, 1, qs],
                                                      start=False, stop=True),
                             reads=[KK_b, qb], writes=[pb[bank]])
                        p, p_b, _ = pT.next()
                        bias_ = self.amask[:, kc, q:q + 1] if S > self.SP else 0.0
                        K.op("act", lambda e: e.activation(out=p[:], in_=ps[:, bank, :], func=AF.Exp, scale=scale, bias=bias_),
                             reads=[pb[bank], self.b_const], writes=[p_b])
                        return p, p_b

                    def pv(kc, p, p_b):
                        K.op("pe", lambda e: e.matmul(ps[:, obank, :], lhsT=V[:, kc, h * 128:(h + 1) * 128], rhs=p[:],
                                                      start=(kc == 0), stop=(kc == nkc - 1)),
                             reads=[V_b, p_b], writes=[pb[obank]])
                        K.op("pe", lambda e: e.matmul(ps[:, dbank, :], lhsT=self.ones[:], rhs=p[:],
                                                      start=(kc == 0), stop=(kc == nkc - 1)),
                             reads=[self.b_const, p_b], writes=[pb[dbank]])

                    LA = 3
                    pend = [scores(kk) for kk in range(min(LA, nkc))]
                    for kc in range(nkc):
                        p, p_b = pend.pop(0)
                        pv(kc, p, p_b)
                        if kc + LA < nkc:
                            pend.append(scores(kc + LA))
                    K.op("dve", lambda e: e.reciprocal(out=rden[:], in_=ps[:, dbank, :]), reads=[pb[dbank]], writes=[rden_b])
                    o, o_b, o_d = ot.next()
                    K.op("dve", lambda e: e.tensor_tensor(out=o[:], in0=ps[:, obank, :], in1=rden[:], op=ALU.mult),
                         reads=[pb[obank], rden_b], writes=[o_b])
                    K.dma(self.SQ, self.ATS[h, :, qs], o[:], o_d, reads=[o_b])

    def tm_epilogue(self, ybanks, xres, xres_b, Gt, out, out_b, scr):
        K = self.K
        ps, pb = self.ps, self.pb
        junk, junk_b, ss, ss_b, tmp, tmp_b = scr
        yv = ps[:, ybanks[0]:ybanks[0] + 2, :].rearrange("p a b -> p (a b)")
        ybs = [pb[ybanks[0]], pb[ybanks[1]]]
        K.op("act", lambda e: e.activation(out=junk[:], in_=yv, func=AF.Square, accum_out=ss[:, 0:1]),
             reads=ybs, writes=[junk_b, ss_b])
        self.rstd_from_ss(ss[:, 0:1], ss[:, 1:2], D, ss_b, ss_b)
        K.op("dve", lambda e: e.scalar_tensor_tensor(out=tmp[:], in0=yv, scalar=ss[:, 1:2], in1=Gt, op0=ALU.mult, op1=ALU.mult),
             reads=ybs + [ss_b, self.G_b], writes=[tmp_b])
        K.op("pool", lambda e: e.tensor_tensor(out=out, in0=tmp[:], in1=xres, op=ALU.add),
             reads=[tmp_b, xres_b], writes=[out_b])

    def ph4(self, S, off, l, xin):
        nc, K = self.nc, self.K
        ps, pb = self.ps, self.pb
        nkc = S // 128
        with contextlib.ExitStack() as st:
            A = lambda n, s, d=F32: st.enter_context(nc.sbuf_tensor(self.nm(n), s, d))
            xt = Pool(K, st, "p4x", 1, [128, 4, D], F32)
            xo = Pool(K, st, "p4xo", 4, [128, D], F32)
            yz = Pool(K, st, "p4yz", 2, [128, 4, D], BF16)
            dcs = Pool(K, st, "p4dc", 2, [128, 2, 4, 512], BF16)
            h2p = Pool(K, st, "p4h2", 1, [128, 8, 512], BF16)
            h2p.b = [MB() for _ in h2p.b]
            AT = A("p4AT", [128, 8, 512], BF16); AT_b = Buf(); AT_d = self.dsem("p4AT")
            GT = A("p4GT", [128, 16, 512], BF16); GT_b = Buf(); GT_d = self.dsem("p4GT")
            FT = A("p4FT", [128, 4, 512], BF16); FT_b = Buf()
            mT = A("p4mT", [128, 8, 512], BF16); mT_b = Buf()
            t1 = A("p4t1", [128, 512]); t1_b = Buf()
            t2 = A("p4t2", [128, 512]); t2_b = Buf()
            junk = A("p4junk", [128, D], BF16); junk_b = Buf()
            ssp = Pool(K, st, "p4ss", 4, [128, 2], F32, dma=False)
            tmpp = Pool(K, st, "p4tmp", 2, [128, D], F32, dma=False)
            ss2p = Pool(K, st, "p4ss2", 4, [128, 2], F32, dma=False)
            xnp = Pool(K, st, "p4xn", 2, [128, D], F32, dma=False)
            dfc, dfs = self.dft_c[S], self.dft_s[S]
            wa = A("p4wa", [128, 8, D], BF16); wab = Buf()
            wo = A("p4wo", [128, 8, D], BF16); wob = Buf()
            wf = A("p4wf", [128, 4, D], BF16); wfb = Buf()
            for hk in range(2):
                K.dma("sp", wa[:, hk * 4:hk * 4 + 4, :], self.WAO[l, hk * 512:(hk + 1) * 512, :].rearrange("(k p) m -> p k m", p=128),
                      self.dsem("p4wa"), writes=[wab])
                K.dma("sp", wo[:, hk * 4:hk * 4 + 4, :], self.WO[l, hk * 512:(hk + 1) * 512, :].rearrange("(k p) m -> p k m", p=128),
                      self.dsem("p4wo"), writes=[wob])
            K.dma("sp", wf[:], self.WF[l].rearrange("(k p) m -> p k m", p=128), self.dsem("p4wf"), writes=[wfb])
            for t in range(S // 512):
                t0 = off + t * 512
                ts = t * 512
                seg = t0 // self.SP
                sloc = ts // self.SP
                for n4 in range(nkc // 4):
                    yzt, yzb, yzd = yz.next()
                    K.dma("sp", yzt[:], self.YZS[n4 * 512:(n4 + 1) * 512, :].rearrange("(c p) d -> p c d", p=128), yzd, writes=[yzb])
                    dct, dcb, dcd = dcs.next()
                    K.dma("sp", dct[:, 0, :, :], dfc[n4 * 512:(n4 + 1) * 512, ts:ts + 512].rearrange("(c p) k -> p c k", p=128), dcd, writes=[dcb])
                    K.dma("sp", dct[:, 1, :, :], dfs[n4 * 512:(n4 + 1) * 512, ts:ts + 512].rearrange("(c p) k -> p c k", p=128), dcd, writes=[dcb])
                    for g in range(4):
                        for c in range(4):
                            n = n4 * 4 + c
                            for cs in range(2):
                                K.op("pe", lambda e: e.matmul(ps[:, g, :], lhsT=yzt[:, c, g * 256 + cs * 128:g * 256 + (cs + 1) * 128],
                                                              rhs=dct[:, cs, c, :], start=(n == 0 and cs == 0),
                                                              stop=(n == nkc - 1 and cs == 1)),
                                     reads=[yzb, dcb], writes=[pb[g]])
                    if n4 == 0:
                        for q4 in range(2):
                            K.dma("sp", AT[:, q4 * 4:q4 * 4 + 4, :], self.ATS[q4 * 4:q4 * 4 + 4, :, ts:ts + 512].rearrange("h p t -> p h t"), AT_d, writes=[AT_b])
                        for q4 in range(4):
                            K.dma("sp", GT[:, q4 * 4:q4 * 4 + 4, :], self.GS[q4 * 4:q4 * 4 + 4, :, ts:ts + 512].rearrange("c p t -> p c t"), GT_d, writes=[GT_b])
                        xtile, xb_, xd = xt.next()
                        K.dma("sp", xtile[:], xin[t0:t0 + 512, :].rearrange("(j p) d -> p j d", p=128), xd, writes=[xb_])
                for g in range(4):
                    K.op("act", lambda e: e.activation(out=FT[:, g, :], in_=ps[:, g, :], func=AF.Copy), reads=[pb[g]], writes=[FT_b])
                for hf in range(2):
                    for mm in range(4):
                        m = hf * 4 + mm
                        ba = 4 + (m % 2) * 2
                        bb = ba + 1
                        for kc in range(8):
                            K.op("pe", lambda e: e.matmul(ps[:, ba, :], lhsT=wa[:, kc, m * 128:(m + 1) * 128], rhs=AT[:, kc, :],
                                                          start=(kc == 0), stop=(kc == 7)),
                                 reads=[wab, AT_b], writes=[pb[ba]])
                        for kc in range(4):
                            K.op("pe", lambda e: e.matmul(ps[:, bb, :], lhsT=wf[:, kc, m * 128:(m + 1) * 128], rhs=FT[:, kc, :],
                                                          start=(kc == 0), stop=(kc == 3)),
                                 reads=[wfb, FT_b], writes=[pb[bb]])
                        K.op("dve", lambda e: e.tensor_tensor(out=t1[:], in0=ps[:, ba, :], in1=GT[:, m, :], op=ALU.mult),
                             reads=[pb[ba], GT_b], writes=[t1_b])
                        K.op("dve", lambda e: e.tensor_tensor(out=t2[:], in0=ps[:, bb, :], in1=GT[:, 8 + m, :], op=ALU.mult),
                             reads=[pb[bb], GT_b], writes=[t2_b])
                        K.op("pool", lambda e: e.tensor_tensor(out=mT[:, m, :], in0=t1[:], in1=t2[:], op=ALU.add),
                             reads=[t1_b, t2_b], writes=[mT_b])
                h2t, h2b, h2d = h2p.next()
                xouts = []
                for jp in range(2):
                    for hf in range(2):
                        for jj in range(2):
                            j = jp * 2 + jj
                            bank = jj * 2 + hf
                            for kc in range(8):
                                K.op("pe", lambda e: e.matmul(ps[:, bank, :], lhsT=mT[:, kc, j * 128:(j + 1) * 128], rhs=wo[:, kc, hf * 512:(hf + 1) * 512],
                                                              start=(kc == 0), stop=(kc == 7)),
                                     reads=[wob, mT_b], writes=[pb[bank]])
                    for jj in range(2):
                        j = jp * 2 + jj
                        xot, xob, xod = xo.next()
                        ss, ss_b, _ = ssp.next()
                        tmp, tmp_b, _ = tmpp.next()
                        self.tm_epilogue([jj * 2, jj * 2 + 1], xtile[:, j, :], xb_, self.G[:, sloc, 0, :], xot[:], xob,
                                         (junk, junk_b, ss, ss_b, tmp, tmp_b))
                        K.dma(self.SQ, self.xb[t0 + j * 128:t0 + (j + 1) * 128, :], xot[:], xod, reads=[xob])
                        xouts.append((xot, xob))
                for j in range(4):
                    xot, xob = xouts[j]
                    ss2, ss2_b, _ = ss2p.next()
                    xn, xn_b, _ = xnp.next()
                    self.norm_T(xot[:], xob, h2t, h2b, j * 128,
                                lambda kc: self.A2[:, l, seg, kc:kc + 1], lambda kc: self.modF[:, l, 24 + kc, seg:seg + 1],
                                (junk, junk_b, ss2, ss2_b, xn, xn_b), split_evac=True)
                for q4 in range(2):
                    K.dma(self.SQ, self.H2S[q4 * 4:q4 * 4 + 4, :, 1 + ts:1 + ts + 512].rearrange("c p t -> p c t"), h2t[:, q4 * 4:q4 * 4 + 4, :], h2d, reads=[h2b])

    def ph5(self, S, off, l):
        nc, K = self.nc, self.K
        ps, pb = self.ps, self.pb
        with contextlib.ExitStack() as st:
            A = lambda n, s, d=F32: st.enter_context(nc.sbuf_tensor(self.nm(n), s, d))
            self.wpool = Pool(K, st, "wst", self.NW, [128, 4096], BF16)
            xt = Pool(K, st, "p5x", 2, [128, 4, D], F32)
            xo = Pool(K, st, "p5xo", 2, [128, D], F32)
            h2p = Pool(K, st, "p5h2", 2, [128, 8, 514], BF16)
            halo = A("p5halo", [128, NCH, 2]); halo_b = Buf()
            up = Pool(K, st, "p5up", 3, [128, 2, 514], F32, dma=False)
            cv = Pool(K, st, "p5cv", 3, [128, 2, 512], F32, dma=False)
            glp = Pool(K, st, "p5gl", 2, [128, 512], F32, dma=False)
            up.b = [(MB(), MB()) for _ in up.b]
            cv.b = [(Buf(), Buf()) for _ in cv.b]
            gT = A("p5gT", [128, 22, 512], BF16); gT_b = Buf()
            junk = A("p5junk", [128, D], BF16); junk_b = Buf()
            ssp = Pool(K, st, "p5ss", 4, [128, 2], F32, dma=False)
            tmpp = Pool(K, st, "p5tmp", 2, [128, D], F32, dma=False)
            WUPl, WDNl = self.WUP[l], self.WDN[l]
            hbufs = [Buf() for _ in range(4)]
            upr = [0]
            wc = lambda k, c: self.wconv[:, l, k * NCH + c:k * NCH + c + 1]
            def load_tile(t):
                t0 = off + t * 512
                ts = t * 512
                xtile, xb_, xd = xt.next()
                K.dma("sp", xtile[:], self.xb[t0:t0 + 512, :].rearrange("(j p) d -> p j d", p=128), xd, writes=[xb_])
                h2T, h2T_b, h2d = h2p.next()
                c_lo = 1 if t == 0 else 0
                c_hi = 513 if t == S // 512 - 1 else 514
                for hh_ in range(2):
                    K.dma("sp", h2T[:, hh_ * 4:hh_ * 4 + 4, c_lo:c_hi],
                          self.H2S[hh_ * 4:hh_ * 4 + 4, :, ts + c_lo:ts + c_hi].rearrange("c p t -> p c t"), h2d, writes=[h2T_b])
                if t == 0:
                    K.op("dve", lambda e: e.memset(h2T[:, :, 0:1], 0.0), writes=[h2T_b])
                if t == S // 512 - 1:
                    K.op("dve", lambda e: e.memset(h2T[:, :, 513:514], 0.0), writes=[h2T_b])
                return xtile, xb_, h2T, h2T_b

            nxt = load_tile(0)
            pend_g = []

            def flush_gelu():
                while pend_g:
                    pend_g.pop(0)()

            for t in range(S // 512):
                t0 = off + t * 512
                ts = t * 512
                sloc = ts // self.SP
                xtile, xb_, h2T, h2T_b = nxt
                hmain = lambda kc: h2T[:, kc, 1:513]
                hhalo = lambda kc: h2T[:, kc, 0:514:513]
                groups = [(g * 4, 4) for g in range(5)] + [(20, 2)]
                for (j0, nj) in groups:
                    wa, wab = self.wload(WUPl[:, j0 * 128:(j0 + nj) * 128].rearrange("(k p) m -> p k m", p=128), (8, nj * 128))
                    wb2, wbb = self.wload(WUPl[:, DFF + j0 * 128:DFF + (j0 + nj) * 128].rearrange("(k p) m -> p k m", p=128), (8, nj * 128))
                    hb = 4
                    for ab, (wt_, wtb) in enumerate(((wa, wab), (wb2, wbb))):
                        for jj in range(nj):
                            ch = j0 + jj + ab * 22
                            for kc in range(8):
                                K.op("pe", lambda e: e.matmul(ps[:, hb, ch * 2:ch * 2 + 2], lhsT=wt_[:, kc, jj * 128:(jj + 1) * 128], rhs=hhalo(kc),
                                                              start=(kc == 0), stop=(kc == 7)),
                                     reads=[wtb, h2T_b], writes=[pb[hb]])
                    for ab in range(2):
                        c0 = (j0 + ab * 22) * 2
                        K.op("act", lambda e: e.activation(out=halo[:, j0 + ab * 22:j0 + ab * 22 + nj, :].rearrange("p a b -> p (a b)"),
                                                           in_=ps[:, hb, c0:c0 + 2 * nj], func=AF.Copy),
                             reads=[pb[hb]], writes=[halo_b])
                        for col, edge in ((0, ts), (1, ts + 512)):
                            if edge % self.SP == 0 and 0 < edge < S:
                                hv = halo[:, j0 + ab * 22:j0 + ab * 22 + nj, col:col + 1]
                                K.op("dve", lambda e: e.tensor_scalar(out=hv, in0=hv, scalar1=self.bfac[:, 0:1], scalar2=None, op0=ALU.mult),
                                     reads=[halo_b, self.b_const], writes=[halo_b])
                    for jj in range(nj):
                        j = j0 + jj
                        upt, upb2, _ = up.next()
                        cvt, cvb2, _ = cv.next()
                        for ab, (wt_, wtb) in enumerate(((wa, wab), (wb2, wbb))):
                            upb, cvb = upb2[ab], cvb2[ab]
                            ch = j + ab * 22
                            bank = 5 + upr[0]
                            upr[0] = (upr[0] + 1) % 3
                            for kc in range(8):
                                K.op("pe", lambda e: e.matmul(ps[:, bank, :], lhsT=wt_[:, kc, jj * 128:(jj + 1) * 128], rhs=hmain(kc),
                                                              start=(kc == 0), stop=(kc == 7)),
                                     reads=[wtb, h2T_b], writes=[pb[bank]])
                            K.op("act", lambda e: e.activation(out=upt[:, ab, 1:513], in_=ps[:, bank, :], func=AF.Copy),
                                 reads=[pb[bank]], writes=[upb])
                            K.op("dve", lambda e: e.tensor_copy(out=upt[:, ab, 0:514:513], in_=halo[:, ch, :]),
                                 reads=[halo_b], writes=[upb])
                            if ab == 0:
                                K.op("act", lambda e: e.activation(out=cvt[:, ab, :], in_=ps[:, bank, :], func=AF.Identity,
                                                                   scale=wc(1, ch), bias=self.bconv[:, l, ch:ch + 1]),
                                     reads=[pb[bank], self.b_vec], writes=[cvb])
                                flush_gelu()
                            else:
                                K.op("pool", lambda e: e.tensor_scalar(out=cvt[:, ab, :], in0=upt[:, ab, 1:513], scalar1=wc(1, ch),
                                                                       scalar2=self.bconv[:, l, ch:ch + 1], op0=ALU.mult, op1=ALU.add),
                                     reads=[upb, self.b_vec], writes=[cvb])
                            K.op("dve", lambda e: e.scalar_tensor_tensor(out=cvt[:, ab, :], in0=upt[:, ab, 0:512], scalar=wc(0, ch),
                                                                         in1=cvt[:, ab, :], op0=ALU.mult, op1=ALU.add),
                                 reads=[upb, cvb, self.b_vec], writes=[cvb])
                            K.op("dve", lambda e: e.scalar_tensor_tensor(out=cvt[:, ab, :], in0=upt[:, ab, 2:514], scalar=wc(2, ch),
                                                                         in1=cvt[:, ab, :], op0=ALU.mult, op1=ALU.add),
                                 reads=[upb, cvb, self.b_vec], writes=[cvb])
                        def gelu_(cvt=cvt, cvb2=cvb2, j=j):
                            gl, gl_b, _ = glp.next()
                            K.op("act", lambda e: e.activation(out=gl[:], in_=cvt[:, 0, :], func=AF.Gelu_apprx_tanh),
                                 reads=[cvb2[0]], writes=[gl_b])
                            K.op("pool", lambda e: e.tensor_tensor(out=gT[:, j, :], in0=gl[:], in1=cvt[:, 1, :], op=ALU.mult),
                                 reads=[gl_b, cvb2[1]], writes=[gT_b])
                        pend_g.append(gelu_)
                flush_gelu()
                if t + 1 < S // 512:
                    nxt = load_tile(t + 1)
                kgroups = [(g * 4, 4) for g in range(5)] + [(20, 2)]
                for jp in range(2):
                    for (k0, nk) in kgroups:
                        wd, wdb = self.wload(WDNl[k0 * 128:(k0 + nk) * 128, :].rearrange("(k p) m -> p k m", p=128), (nk, 1024))
                        for jj in range(2):
                            j = jp * 2 + jj
                            for hf in range(2):
                                bank = jj * 2 + hf
                                for kk in range(nk):
                                    kc = k0 + kk
                                    K.op("pe", lambda e: e.matmul(ps[:, bank, :], lhsT=gT[:, kc, j * 128:(j + 1) * 128],
                                                                  rhs=wd[:, kk, hf * 512:(hf + 1) * 512], start=(kc == 0), stop=(kc == 21)),
                                         reads=[wdb, gT_b], writes=[pb[bank]])
                    for jj in range(2):
                        j = jp * 2 + jj
                        xot, xob, xod = xo.next()
                        ss, ss_b, _ = ssp.next()
                        tmp, tmp_b, _ = tmpp.next()
                        self.tm_epilogue([jj * 2, jj * 2 + 1], xtile[:, j, :], xb_, self.G[:, sloc, 1, :], xot[:], xob,
                                         (junk, junk_b, ss, ss_b, tmp, tmp_b))
                        K.dma(self.SQ, self.xout[t0 + j * 128:t0 + (j + 1) * 128, :], xot[:], xod, reads=[xob])


def host_consts(slots, seg, split_big):
    c = {}
    c["ident"] = np.eye(128, dtype=np.float32)
    dd = np.arange(128)
    c["sel"] = (dd[:, None] % 64 == dd[None, :] % 64).astype(np.float32)
    half = 32
    inv = (1.0 / (10000.0 ** (np.arange(half, dtype=np.float32) / np.float32(half)))).astype(np.float32)

    def rope_cols(S):
        ang = (np.arange(S, dtype=np.float32)[:, None] * inv[None, :]).astype(np.float32)
        cs, sn = np.cos(ang).T.astype(np.float32), np.sin(ang).T.astype(np.float32)
        return np.concatenate([cs, cs, -sn, sn], axis=0)

    cols = []
    for si, S in enumerate(slots):
        if si == 0 and split_big:
            cols += [rope_cols(seg)] * (S // seg)
        else:
            cols.append(rope_cols(S))
    c["rope_t"] = np.ascontiguousarray(np.concatenate(cols, axis=1).astype(np.float32))
    SB = max(slots)
    am = np.zeros((128, SB // 128, SB // 512), np.float32)
    if split_big:
        kseg = (np.arange(SB // 128) * 128) // seg
        qseg = (np.arange(SB // 512) * 512) // seg
        am[:, kseg[:, None] != qseg[None, :]] = -30000.0
    c["amask"] = am
    c["bfac"] = np.full((128, 1), 0.0 if split_big else 1.0, np.float32)
    ch = np.arange(128, dtype=np.float64)
    ach = 2.0 * np.pi * ((ch[:, None] * ch[None, :]) % 128) / 128.0

    def dft(S):
        n = np.arange(S, dtype=np.int64)
        a = (2.0 * np.pi / S) * ((n[:, None] * n[None, :]) % S).astype(np.float64)
        return np.cos(a), -np.sin(a)

    for S in sorted(set(slots)):
        eff = seg if (S == slots[0] and split_big) else S
        nrm = 1.0 / np.sqrt(128.0 * eff)
        c[f"csc{S}"] = np.concatenate([np.cos(ach) * nrm, np.sin(ach) * nrm], axis=1).astype(ml_dtypes.bfloat16)
        if eff == S:
            dc, dsn = dft(S)
        else:
            bc, bs = dft(eff)
            dc = np.zeros((S, S)); dsn = np.zeros((S, S))
            for b in range(S // eff):
                dc[b * eff:(b + 1) * eff, b * eff:(b + 1) * eff] = bc
                dsn[b * eff:(b + 1) * eff, b * eff:(b + 1) * eff] = bs
        c[f"dftc{S}"] = dc.astype(ml_dtypes.bfloat16)
        c[f"dfts{S}"] = dsn.astype(ml_dtypes.bfloat16)
    return c


WNAMES = ["w_ada", "b_ada", "g_mix_pre", "g_mix_post", "w_in", "g_q", "w_q_b", "g_kv", "w_kv_b", "w_attn_o",
          "w_four", "w_out", "g_ffn_pre", "g_ffn_post", "w_up", "w_conv", "b_conv", "w_down"]

_CACHE = {}


def get_prog(cfg_key, cfg):
    if cfg_key not in _CACHE:
        _CACHE[cfg_key] = Prog(cfg)
    return _CACHE[cfg_key]


def kernel(x_prompt, x_sample, c_prompt, c_sample, **weights):
    x_prompt = np.asarray(x_prompt, np.float32)
    x_sample = np.asarray(x_sample, np.float32)
    c_prompt = np.asarray(c_prompt, np.float32)
    c_sample = np.asarray(c_sample, np.float32)
    L = int(np.asarray(weights["w_in"]).shape[0])
    SB, SPL = x_sample.shape[1], x_prompt.shape[1]
    nsmp, nprm = x_sample.shape[0], x_prompt.shape[0]
    nbig = SB // SPL
    nB = N_CORES - nsmp
    npp = (nprm - nB * nbig) // N_CORES
    assert nsmp * npp + nB * (npp + nbig) == nprm and SB % SPL == 0
    slots = [SB] + [SPL] * npp
    cfg = {"depth": L, "slots": slots, "seg": SPL}
    prog = get_prog(("full", L, tuple(slots)), cfg)
    cA = host_consts(slots, SPL, False)
    cB = host_consts(slots, SPL, True)
    wts = {k: np.ascontiguousarray(np.asarray(weights[k], np.float32)) for k in WNAMES}
    in_maps, owner = [], []
    nxt = 0
    for core in range(N_CORES):
        if core < nsmp:
            xs = [x_sample[core]]
            cs = [c_sample[core]] * nbig
            own = [("s", core)]
        else:
            ids = list(range(nxt, nxt + nbig))
            nxt += nbig
            xs = [np.concatenate([x_prompt[i] for i in ids], axis=0)]
            cs = [c_prompt[i] for i in ids]
            own = [("pb", ids)]
        ids = list(range(nxt, nxt + npp))
        nxt += npp
        xs += [x_prompt[i] for i in ids]
        cs += [c_prompt[i] for i in ids]
        own += [("p", i) for i in ids]
        m = {"x_all": np.ascontiguousarray(np.concatenate(xs, axis=0)),
             "c_all": np.ascontiguousarray(np.stack(cs, axis=0))}
        m.update(cA if core < nsmp else cB)
        m.update(wts)
        in_maps.append(m)
        owner.append(own)
    res = run_bass_kernel_spmd(prog.nc, in_maps, core_ids=list(range(N_CORES)))
    y_prompt = np.empty_like(x_prompt)
    y_sample = np.empty_like(x_sample)
    for core in range(N_CORES):
        y = np.asarray(res.results[core]["y_all"], np.float32)
        off = 0
        for kind, idx in owner[core]:
            if kind == "s":
                y_sample[idx] = y[off:off + SB]
                off += SB
            elif kind == "pb":
                for i in idx:
                    y_prompt[i] = y[off:off + SPL]
                    off += SPL
            else:
                y_prompt[idx] = y[off:off + SPL]
                off += SPL
    return (y_prompt, y_sample)
```
